# Optimizing a Trainium2 kernel written in Bass

```python
import jax, jax.numpy as jnp
from jax import lax
import numpy as np


D_MODEL = 2048
BATCH = 4
SEQ = 2048
DEPTH = 4
DEC_BATCH = 128
DEC_SEQ = 4
PAST_LEN = 16384
PAGE_SIZE = 128

HG_HEADS = 8
HG_DK = 128
HG_DV = 128
HG_WIDTH = HG_HEADS * HG_DK
HG_VWIDTH = HG_HEADS * HG_DV
POOL_WINDOWS = (2, 4, 8, 16)
POOL_GROUPS = len(POOL_WINDOWS)
POOL_GC = 128
POOL_WIDTH = POOL_GROUPS * POOL_GC
POOL_BUF = max(POOL_WINDOWS) - 1
XA_HEADS = 4
XA_DH = 128
XA_WIDTH = XA_HEADS * XA_DH
N_MEM = 256
N_BRANCH = 3
MIX_WIDTH = HG_VWIDTH + POOL_WIDTH + XA_WIDTH
D_FF = 5632
CHUNK = 64
EPS = 1e-6

IN_SIZES = (HG_WIDTH, HG_WIDTH, HG_VWIDTH, HG_VWIDTH, POOL_WIDTH, XA_WIDTH, N_BRANCH * D_MODEL)
IN_WIDTH = int(sum(IN_SIZES))
IN_OFFSETS = [int(v) for v in np.cumsum(IN_SIZES)[:-1]]
BR_OFFSETS = [HG_VWIDTH, HG_VWIDTH + POOL_WIDTH]

kernel_name = "hgrn2_pool_memxattn_macaron_gated_trunk_step"


def rmsnorm(x, g):
    xf = x.astype(jnp.float32)
    y = xf * lax.rsqrt(jnp.mean(xf * xf, axis=-1, keepdims=True) + EPS)
    return (y * g.astype(jnp.float32)).astype(x.dtype)


def swiglu(x, w1, w3, w2):
    return (jax.nn.silu(x @ w1) * (x @ w3)) @ w2


def hgrn_scan(q, k, v, logf, S0):
    B, T, H, _ = q.shape
    C = CHUNK if T % CHUNK == 0 else T
    nc = T // C
    f32 = jnp.float32

    def to_chunks(a):
        return jnp.moveaxis(a.astype(f32).reshape(B, nc, C, *a.shape[2:]), 1, 0)

    mask = np.tril(np.ones((C, C), dtype=bool))[None, :, :, None, None]

    def step(S, inp):
        qc, kc, vc, gc = inp
        b = jnp.cumsum(gc, axis=1)
        diff = b[:, :, None] - b[:, None, :]
        dec = jnp.exp(jnp.where(mask, diff, -jnp.inf))
        A = jnp.einsum('bthk,bjhk,btjhk->bhtj', qc, kc, dec)
        o = (jnp.einsum('bhtj,bjhv->bthv', A, vc)
             + jnp.einsum('bthk,bhkv->bthv', qc * jnp.exp(b), S))
        bl = b[:, -1]
        kd = kc * jnp.exp(bl[:, None] - b)
        S = S * jnp.exp(bl)[..., None] + jnp.einsum('bjhk,bjhv->bhkv', kd, vc)
        return S, o

    S, o = lax.scan(step, S0.astype(f32), (to_chunks(q), to_chunks(k), to_chunks(v), to_chunks(logf)))
    o = jnp.moveaxis(o, 0, 1).reshape(B, T, H, v.shape[-1])
    return o, S


def pool_mix(u, buf, w_pool_l, scale_l):
    B, T, _ = u.shape
    P = buf.shape[1]
    uu = jnp.concatenate([buf.astype(u.dtype), u], axis=1)
    cs = jnp.pad(jnp.cumsum(uu.astype(jnp.float32), axis=1), ((0, 0), (1, 0), (0, 0)))
    n = P + np.arange(T)
    outs = []
    for g, w in enumerate(POOL_WINDOWS):
        lo = np.maximum(n + 1 - w, 0)
        cnt = (n + 1 - lo).astype(np.float32)
        sl = slice(g * POOL_GC, (g + 1) * POOL_GC)
        s = cs[:, n + 1, sl] - cs[:, lo, sl]
        outs.append(s / cnt[None, :, None] - u[..., sl].astype(jnp.float32))
    d = jnp.stack(outs, axis=2)
    y = jnp.einsum('btgc,gcd->btgd', d, w_pool_l.astype(jnp.float32)).reshape(B, T, POOL_WIDTH)
    y = y * scale_l.astype(jnp.float32)
    return y.astype(u.dtype), uu[:, -POOL_BUF:]


def mixer(h, S0, buf, mk, mv, w_in_l, lb_l, hg_norm_l, w_pool_l, pool_scale_l, w_branch_l, w_out_l):
    B, T, _ = h.shape
    z = h @ w_in_l
    zq, zf, zi, zog, zu, zx, zg = jnp.split(z, IN_OFFSETS, axis=-1)
    q = jax.nn.silu(zq).reshape(B, T, HG_HEADS, HG_DK)
    lb = lb_l.reshape(HG_HEADS, HG_DK)
    fg = lb + (1.0 - lb) * jax.nn.sigmoid(zf.astype(jnp.float32).reshape(B, T, HG_HEADS, HG_DK))
    k = 1.0 - fg
    logf = jnp.log(fg)
    v = zi.reshape(B, T, HG_HEADS, HG_DV)
    oh, S = hgrn_scan(q, k, v, logf, S0)
    oh = oh * lax.rsqrt(jnp.mean(oh * oh, axis=-1, keepdims=True) + EPS)
    oh = oh.reshape(B, T, HG_VWIDTH) * hg_norm_l.astype(jnp.float32)
    oh = (oh * jax.nn.silu(zog.astype(jnp.float32))).astype(h.dtype)
    op, new_buf = pool_mix(zu, buf, w_pool_l, pool_scale_l)
    xq = zx.reshape(B, T, XA_HEADS, XA_DH)
    s = jnp.einsum('bthd,bmhd->bhtm', xq, mk).astype(jnp.float32) * (XA_DH ** -0.5)
    p = jax.nn.softmax(s, axis=-1).astype(h.dtype)
    ox = jnp.einsum('bhtm,bmhd->bthd', p, mv).reshape(B, T, XA_WIDTH)
    gates = jax.nn.sigmoid(zg.astype(jnp.float32)).reshape(B, T, N_BRANCH, D_MODEL)
    wb_h, wb_p, wb_x = jnp.split(w_branch_l, BR_OFFSETS, axis=0)
    y = (gates[:, :, 0] * (oh @ wb_h).astype(jnp.float32)
         + gates[:, :, 1] * (op @ wb_p).astype(jnp.float32)
         + gates[:, :, 2] * (ox @ wb_x).astype(jnp.float32))
    return y.astype(h.dtype) @ w_out_l, S, new_buf


def setup_inputs(seed: int = 0) -> dict:
    key = jax.random.key(seed)
    ks = iter(jax.random.split(key, 40))
    nrm = lambda shape, s: jax.random.normal(next(ks), shape, jnp.float32) * s
    gain = lambda shape: 1.0 + 0.02 * jax.random.normal(next(ks), shape, jnp.float32)
    w_branch = jnp.concatenate([
        nrm((DEPTH, HG_VWIDTH, D_MODEL), HG_VWIDTH ** -0.5),
        nrm((DEPTH, POOL_WIDTH, D_MODEL), POOL_WIDTH ** -0.5),
        nrm((DEPTH, XA_WIDTH, D_MODEL), XA_WIDTH ** -0.5)], axis=1)
    return {
        "x_prompt": nrm((BATCH, SEQ, D_MODEL), 1.0),
        "x_sample": nrm((DEC_BATCH, DEC_SEQ, D_MODEL), 1.0),
        "state_hgrn": nrm((DEPTH, DEC_BATCH, HG_HEADS, HG_DK, HG_DV), 0.3),
        "state_pool": nrm((DEPTH, DEC_BATCH, POOL_BUF, POOL_WIDTH), 1.0),
        "cache_mem_k": nrm((DEPTH, DEC_BATCH, N_MEM, XA_HEADS, XA_DH), 1.0),
        "cache_mem_v": nrm((DEPTH, DEC_BATCH, N_MEM, XA_HEADS, XA_DH), 1.0),
        "mem_prompt": nrm((BATCH, N_MEM, D_MODEL), 1.0),
        "ffn1_norm": gain((DEPTH, D_MODEL)),
        "ffn1_w1": nrm((DEPTH, D_MODEL, D_FF), D_MODEL ** -0.5),
        "ffn1_w3": nrm((DEPTH, D_MODEL, D_FF), D_MODEL ** -0.5),
        "ffn1_w2": nrm((DEPTH, D_FF, D_MODEL), D_FF ** -0.5),
        "mix_norm": gain((DEPTH, D_MODEL)),
        "w_in": nrm((DEPTH, D_MODEL, IN_WIDTH), D_MODEL ** -0.5),
        "lb_logits": 1.0 + nrm((DEPTH, HG_WIDTH), 0.1),
        "hg_norm": gain((DEPTH, HG_VWIDTH)),
        "w_pool": nrm((DEPTH, POOL_GROUPS, POOL_GC, POOL_GC), POOL_GC ** -0.5),
        "pool_scale": 1.0 + nrm((DEPTH, POOL_WIDTH), 0.1),
        "mem_norm": gain((DEPTH, D_MODEL)),
        "w_mk": nrm((DEPTH, D_MODEL, XA_WIDTH), D_MODEL ** -0.5),
        "w_mv": nrm((DEPTH, D_MODEL, XA_WIDTH), D_MODEL ** -0.5),
        "w_branch": w_branch,
        "w_out": nrm((DEPTH, D_MODEL, D_MODEL), 0.5 * D_MODEL ** -0.5),
        "ffn2_norm": gain((DEPTH, D_MODEL)),
        "ffn2_w1": nrm((DEPTH, D_MODEL, D_FF), D_MODEL ** -0.5),
        "ffn2_w3": nrm((DEPTH, D_MODEL, D_FF), D_MODEL ** -0.5),
        "ffn2_w2": nrm((DEPTH, D_FF, D_MODEL), D_FF ** -0.5),
        "final_norm": gain((D_MODEL,)),
    }


def reference(x_prompt, x_sample, state_hgrn, state_pool, cache_mem_k, cache_mem_v, mem_prompt,
              ffn1_norm, ffn1_w1, ffn1_w3, ffn1_w2, mix_norm, w_in, lb_logits, hg_norm, w_pool,
              pool_scale, mem_norm, w_mk, w_mv, w_branch, w_out, ffn2_norm, ffn2_w1, ffn2_w3,
              ffn2_w2, final_norm):
    lbc = jnp.cumsum(jax.nn.softmax(lb_logits.astype(jnp.float32), axis=0), axis=0)
    lbs = lbc - lbc[0:1]

    xp, xs = x_prompt, x_sample
    Bp = x_prompt.shape[0]
    S0p = jnp.zeros((Bp, HG_HEADS, HG_DK, HG_DV), jnp.float32)
    buf0p = jnp.zeros((Bp, 0, POOL_WIDTH), x_prompt.dtype)
    hs_p, pb_p, mk_p, mv_p, hs_s, pb_s = [], [], [], [], [], []
    for l in range(DEPTH):
        mem_n = rmsnorm(mem_prompt, mem_norm[l])
        mk = (mem_n @ w_mk[l]).reshape(Bp, N_MEM, XA_HEADS, XA_DH)
        mv = (mem_n @ w_mv[l]).reshape(Bp, N_MEM, XA_HEADS, XA_DH)
        mix_w = (w_in[l], lbs[l], hg_norm[l], w_pool[l], pool_scale[l], w_branch[l], w_out[l])

        xp = xp + 0.5 * swiglu(rmsnorm(xp, ffn1_norm[l]), ffn1_w1[l], ffn1_w3[l], ffn1_w2[l])
        xs = xs + 0.5 * swiglu(rmsnorm(xs, ffn1_norm[l]), ffn1_w1[l], ffn1_w3[l], ffn1_w2[l])

        op, Sp, bp = mixer(rmsnorm(xp, mix_norm[l]), S0p, buf0p, mk, mv, *mix_w)
        os_, Ss, bs = mixer(rmsnorm(xs, mix_norm[l]), state_hgrn[l], state_pool[l],
                            cache_mem_k[l], cache_mem_v[l], *mix_w)
        xp = xp + op
        xs = xs + os_

        xp = xp + 0.5 * swiglu(rmsnorm(xp, ffn2_norm[l]), ffn2_w1[l], ffn2_w3[l], ffn2_w2[l])
        xs = xs + 0.5 * swiglu(rmsnorm(xs, ffn2_norm[l]), ffn2_w1[l], ffn2_w3[l], ffn2_w2[l])

        hs_p.append(Sp.astype(state_hgrn.dtype))
        pb_p.append(bp.astype(state_pool.dtype))
        mk_p.append(mk.astype(cache_mem_k.dtype))
        mv_p.append(mv.astype(cache_mem_v.dtype))
        hs_s.append(Ss.astype(state_hgrn.dtype))
        pb_s.append(bs.astype(state_pool.dtype))

    y_prompt = rmsnorm(xp, final_norm)
    y_sample = rmsnorm(xs, final_norm)
    return (y_prompt, y_sample, jnp.stack(hs_p), jnp.stack(pb_p), jnp.stack(mk_p), jnp.stack(mv_p),
            jnp.stack(hs_s), jnp.stack(pb_s))
```

```python
import numpy as np
from contextlib import ExitStack
import concourse.bass as bass
import concourse.mybir as mybir
from concourse.bass_utils import run_bass_kernel_spmd

F32 = mybir.dt.float32
BF16 = mybir.dt.bfloat16
AF = mybir.ActivationFunctionType
ALU = mybir.AluOpType

D = 2048
KC = 16
DFF = 5632
FC = 44
DEPTH = 4
NH = 8
EPS = 1e-6
CH = 32
TMAX = 704
HT = 352
NSEQ_S = 16
POOL_W = (2, 4, 8, 16)
NSLOT = 6
NGEN = 20
GW = 4 * DEPTH * KC + KC
G2W = DEPTH * 8 + DEPTH * 4 + 32
CW = 128 + 32 + 64 + 16 + 64

GROUPS = [(0, 640, 16), (640, 704, 0), (1344, 704, 0)]


class Prog:
    ENGS = ("pe", "act", "dve", "pool", "sp")

    def __init__(self):
        self.ops = []
        self.lastw = {}
        self.readers = {}
        self.stream_last = {}
        self.gen_i = 0

    def add(self, eng, fn, reads=(), writes=(), dma=False, stream=None):
        oid = len(self.ops)
        deps = {}
        lastw = self.lastw
        readers = self.readers
        psr = [k for k in reads if isinstance(k, tuple) and k[0] == "ps"]
        if psr:
            writes = list(writes) + [k for k in psr if k not in writes]
        for k in reads:
            w = lastw.get(k)
            if w is not None:
                deps[w] = "raw"
        for k in writes:
            w = lastw.get(k)
            if w is not None and w not in deps:
                deps[w] = "waw"
            rs = readers.get(k)
            if rs:
                for r in rs:
                    if r not in deps:
                        deps[r] = "war"
        for k in reads:
            rs = readers.get(k)
            if rs is None:
                readers[k] = [oid]
            else:
                rs.append(oid)
        for k in writes:
            lastw[k] = oid
            readers[k] = []
        if dma:
            if stream is None:
                stream = "g%d" % (self.gen_i % NGEN)
                self.gen_i += 1
            prev = self.stream_last.get(stream)
            if prev is not None and prev not in deps:
                deps[prev] = "ser"
            self.stream_last[stream] = oid
        keep = []
        ops = self.ops
        for d, kind in deps.items():
            dop = ops[d]
            if (not dop[3]) and (not dma) and dop[0] == eng and eng == "pe":
                continue
            keep.append(d)
        self.ops.append([eng, fn, keep, dma, stream, False, 0, (tuple(reads), tuple(writes))])
        return oid

    def emit(self, nc, block, es):
        ops = self.ops
        for op in ops:
            for d in op[2]:
                ops[d][5] = True
        cnt = {e: 0 for e in self.ENGS}
        scnt = {}
        for op in ops:
            if op[3]:
                scnt[op[4]] = scnt.get(op[4], 0) + 16
                op[6] = scnt[op[4]]
            elif op[5]:
                cnt[op[0]] += 1
                op[6] = cnt[op[0]]
        sems = {}
        for e in self.ENGS:
            sems[e] = es.enter_context(nc.semaphore("sem_" + e))
        for s in scnt:
            sems["dma_" + s] = es.enter_context(nc.semaphore("semd_" + s))
        per = {e: [] for e in self.ENGS}
        for op in ops:
            per[op[0]].append(op)
        final_streams = dict(scnt)

        self.log = []

        def run_engine(ename, eng):
            waited = {}
            for op in per[ename]:
                self.log.append((ename, [(("dma_" + ops[d][4]) if ops[d][3] else ops[d][0], ops[d][6]) for d in op[2]], op[3], op[4], op[6], op[7]))
                need = {}
                for d in op[2]:
                    dop = ops[d]
                    if dop[3]:
                        sk = "dma_" + dop[4]
                    else:
                        sk = dop[0]
                    v = dop[6]
                    if v > need.get(sk, 0):
                        need[sk] = v
                for sk, v in need.items():
                    if waited.get(sk, 0) >= v:
                        continue
                    eng.wait_ge(sems[sk], v)
                    waited[sk] = v
                ins = op[1](eng)
                if op[3]:
                    ins.then_inc(sems["dma_" + op[4]], 16)
                elif op[5]:
                    ins.then_inc(sems[ename], 1)
            if ename == "sp":
                for s, v in final_streams.items():
                    if waited.get("dma_" + s, 0) < v:
                        eng.wait_ge(sems["dma_" + s], v)
                for e in ("pe", "act", "dve", "pool"):
                    if cnt[e] > 0:
                        eng.wait_ge(sems[e], cnt[e])

        @block.tensor
        def _(e):
            run_engine("pe", e)

        @block.scalar
        def _(e):
            run_engine("act", e)

        @block.vector
        def _(e):
            run_engine("dve", e)

        @block.gpsimd
        def _(e):
            run_engine("pool", e)

        @block.sync
        def _(e):
            run_engine("sp", e)


class Builder:
    def __init__(self, depth=DEPTH, groups=GROUPS, dbg=False, stages=None):
        self.stages = stages
        self.depth = depth
        self.groups = groups
        self.dbg = dbg
        self.nc = bass.Bass("TRN2", target_bir_lowering=False)
        self.p = Prog()
        self.bank_i = 0
        self.slot_i = 0
        self.rot = {}

    def dram_in(self, name, shape, dt=F32):
        return self.nc.dram_tensor(name, list(shape), dt, kind="ExternalInput").ap()

    def dram_out(self, name, shape, dt=F32):
        return self.nc.dram_tensor(name, list(shape), dt, kind="ExternalOutput").ap()

    def sb(self, es, name, shape, dt):
        return es.enter_context(self.nc.sbuf_tensor(name, list(shape), dt))

    def MM(self, out, lhsT, rhs, start, stop, reads, writes):
        self.p.add("pe", lambda e: e.matmul(out, lhsT, rhs, start=start, stop=stop), reads, writes)

    def TR(self, out, in_, ident, reads, writes):
        self.p.add("pe", lambda e: e.transpose(out, in_, ident), reads, writes)

    def ACT(self, out, in_, func, reads, writes, bias=None, scale=None):
        kw = {}
        if bias is not None:
            kw["bias"] = bias
        if scale is not None:
            kw["scale"] = scale
        self.p.add("act", lambda e: e.activation(out=out, in_=in_, func=func, **kw), reads, writes)

    def TT(self, out, in0, in1, op, reads, writes, eng="dve"):
        self.p.add(eng, lambda e: e.tensor_tensor(out=out, in0=in0, in1=in1, op=op), reads, writes)

    def TS(self, out, in0, s1, s2, op0, op1, reads, writes, eng="dve"):
        if s2 is None:
            self.p.add(eng, lambda e: e.tensor_scalar(out=out, in0=in0, scalar1=s1, scalar2=None, op0=op0), reads, writes)
        else:
            self.p.add(eng, lambda e: e.tensor_scalar(out=out, in0=in0, scalar1=s1, scalar2=s2, op0=op0, op1=op1), reads, writes)

    def STT(self, out, in0, scalar, in1, op0, op1, reads, writes):
        self.p.add("dve", lambda e: e.scalar_tensor_tensor(out=out, in0=in0, scalar=scalar, in1=in1, op0=op0, op1=op1), reads, writes)

    def CP(self, out, in_, reads, writes, eng="dve"):
        if eng == "act":
            self.p.add("act", lambda e: e.copy(out=out, in_=in_), reads, writes)
        else:
            self.p.add(eng, lambda e: e.tensor_copy(out=out, in_=in_), reads, writes)

    def MEMSET(self, ap, val, writes, eng="dve"):
        self.p.add(eng, lambda e: e.memset(ap, val), (), writes)

    def DMA(self, out, in_, reads, writes, q="sp", stream=None):
        self.p.add(q, lambda e: e.dma_start(out=out, in_=in_), reads, writes, dma=True, stream=stream)

    def bank(self):
        b = self.bank_i % 4
        self.bank_i += 1
        return b

    def rotate(self, name, n):
        i = self.rot.get(name, 0)
        self.rot[name] = i + 1
        return i % n

    def wload(self, src, ncols):
        s = self.slot_i % NSLOT
        self.slot_i += 1
        dst = self.wring[:, s, 0:ncols]
        self.DMA(dst, src, (), [("w", s)], q="pool", stream="w%d" % s)
        return s

    def tk(self, name, kc, c0, c1):
        ks = []
        for ti, (a, b) in enumerate(self.tiles):
            if c0 < b and c1 > a:
                ks.append((name, kc, ti))
        return ks

    def build(self):
        nc = self.nc
        L = self.depth
        with ExitStack() as es:
            es.enter_context(nc.allow_non_contiguous_dma(reason="small strided state/param loads"))
            self.declare(es)
            self.prologue()
            for gi, g in enumerate(self.groups):
                self.run_group(gi, g)
            block = es.enter_context(nc.Block())
            self.p.emit(nc, block, es)
        return nc

    def declare(self, es):
        d = self.dram_in
        o = self.dram_out
        self.xp = d("xp", [2048, D])
        self.xs = d("xs", [64, D])
        self.memp = d("memp", [256, D])
        self.hs_in = d("hs_in", [DEPTH, NSEQ_S, NH, 128, 128])
        self.pool_in = d("pool_in", [DEPTH, NSEQ_S * 15, 512])
        self.ck_in = d("ck_in", [DEPTH, NSEQ_S, 256, 512])
        self.cv_in = d("cv_in", [DEPTH, NSEQ_S, 256, 512])
        self.gvec = d("gvec", [128, GW])
        self.gvec2 = d("gvec2", [128, G2W])
        self.consts = d("consts", [128, CW])
        self.w1a = d("w1a", [DEPTH, FC, 128, D])
        self.w3a = d("w3a", [DEPTH, FC, 128, D])
        self.w2a = d("w2a", [DEPTH, KC, 128, DFF])
        self.w1b = d("w1b", [DEPTH, FC, 128, D])
        self.w3b = d("w3b", [DEPTH, FC, 128, D])
        self.w2b = d("w2b", [DEPTH, KC, 128, DFF])
        self.win = d("win", [DEPTH, 88, 128, D])
        self.wbr = d("wbr", [DEPTH, KC, 128, D])
        self.wout = d("wout", [DEPTH, KC, 128, D])
        self.wmk = d("wmk", [DEPTH, 4, 128, D])
        self.wmv = d("wmv", [DEPTH, 4, 128, D])
        self.wpool = d("wpool", [DEPTH, 4, 128, 128])
        self.yp = o("yp", [2048, D])
        self.ys = o("ys", [64, D])
        self.hsp = o("hsp", [DEPTH, NH, 128, 128])
        self.ppool = o("ppool", [DEPTH, 15, 512])
        self.mk_out = o("mk_out", [DEPTH, 256, 512])
        self.mv_out = o("mv_out", [DEPTH, 256, 512])
        self.hss = o("hss", [DEPTH, NSEQ_S, NH, 128, 128])
        self.pools = o("pools", [DEPTH, NSEQ_S, 15, 512])
        self.handS = self.nc.dram_tensor("handS", [DEPTH, NH, 128, 128], F32).ap()
        sb = self.sb
        self.x = sb(es, "x", [128, KC, TMAX], F32)
        self.xn = sb(es, "xn", [128, KC, TMAX], BF16)
        self.obr = sb(es, "obr", [128, KC, TMAX], BF16)
        self.hy = sb(es, "hy", [128, KC, TMAX], BF16)
        self.wring = sb(es, "wring", [128, NSLOT, 2048], BF16)
        self.sf = [sb(es, "sf%d" % i, [128, TMAX + 16], F32) for i in range(5)]
        self.sbb = [sb(es, "sbb%d" % i, [128, TMAX], BF16) for i in range(4)]
        self.tokk = sb(es, "tokk", [64, 2, 4, 128], BF16)
        self.tokv = sb(es, "tokv", [64, 2, 4, 128], BF16)
        self.tokks = sb(es, "tokks", [64, 128], BF16)
        self.tokvs = sb(es, "tokvs", [64, 128], BF16)
        self.stg = sb(es, "stg", [128, 2, 1024], F32)
        self.sq = sb(es, "sq", [128, 2, HT], BF16)
        self.rs = sb(es, "rs", [128, 2, HT], F32)
        self.sa = sb(es, "sa", [128, 2, HT], F32)
        self.g_sb = sb(es, "g_sb", [128, GW], F32)
        self.g2_sb = sb(es, "g2_sb", [128, G2W], F32)
        self.c_sb = sb(es, "c_sb", [128, CW], F32)
        self.identb = sb(es, "identb", [128, 128], BF16)
        self.onesb = sb(es, "onesb", [128, 128], BF16)
        self.epsc = sb(es, "epsc", [128, 1], F32)
        self.rmask = sb(es, "rmask", [128, 768], BF16)
        self.lb = sb(es, "lb", [128, 8, 4], F32)
        self.oml = sb(es, "oml", [128, 8, 4], F32)
        self.lbtmp = sb(es, "lbtmp", [128, 8, 4], F32)
        self.lbs = sb(es, "lbs", [128, 8], F32)
        self.dd = sb(es, "dd", [128, 2, 40], F32)
        self.Sf = sb(es, "Sf", [128, 2, 128], F32)
        self.Sb = sb(es, "Sb", [128, 4, 128], BF16)
        self.atsb = sb(es, "atsb", [64, 2, 64], BF16)
        self.ptail = sb(es, "ptail", [128, DEPTH, 4, 16], F32)
        self.mkT = sb(es, "mkT", [128, 4, 256], BF16)
        self.mvt = sb(es, "mvt", [128, 2, 512], BF16)
        self.ckb = sb(es, "ckb", [128, 2, 2, 512], BF16)
        self.cvb = sb(es, "cvb", [128, 2, 2, 512], BF16)
        self.mkTs = sb(es, "mkTs", [128, 4, 256], BF16)
        self.S0f = sb(es, "S0f", [128, 2, 4, 128], F32)
        self.S0b = sb(es, "S0b", [128, 2, 4, 128], BF16)
        self.kmask = sb(es, "kmask", [64, 2, 4, 128], BF16)
        self.uus = sb(es, "uus", [128, NSEQ_S, 19], F32)
        self.uu2 = sb(es, "uu2", [128, NSEQ_S, 19], F32)
        self.ss1 = sb(es, "ss1", [128, 8], F32)
        self.ss2 = sb(es, "ss2", [128, 2], F32)
        self.ps = [es.enter_context(self.nc.psum_tensor("ps%d" % i, [128, 512], F32)) for i in range(8)]

    def ident_f(self, n=128):
        return self.c_sb[0:n, 0:n]

    def prologue(self):
        self.DMA(self.g_sb[:], self.gvec[:, :], (), ["g_sb"])
        self.DMA(self.g2_sb[:], self.gvec2[:, :], (), ["g2_sb"])
        self.DMA(self.c_sb[:], self.consts[:, :], (), ["c_sb"])
        self.CP(self.identb[:], self.c_sb[:, 0:128], ["c_sb"], ["identb"])
        self.MEMSET(self.onesb[:], 1.0, ["onesb"])
        self.MEMSET(self.epsc[:], EPS, ["epsc"])
        self.MEMSET(self.ptail[:], 0.0, ["ptail"])
        for i in range(5):
            self.MEMSET(self.sf[i][:], 0.0, [("sf", i)])
        for i in range(4):
            self.MEMSET(self.sbb[i][:], 0.0, [("sbb", i)])
        self.MEMSET(self.uus[:], 0.0, ["uus"])
        self.MEMSET(self.uu2[:], 0.0, ["uu2"])
        self.MEMSET(self.dd[:], 0.0, [("dd", 0), ("dd", 1)])
        self.MEMSET(self.rmask[:], 1.0, ["rmask"])
        self.MEMSET(self.rmask[:, 0:704].rearrange("p (c j) -> p c j", j=CH)[:, :, 0:1], 0.0, ["rmask"])
        self.MEMSET(self.rmask[:, 704:768].rearrange("p (c j) -> p c j", j=4)[:, :, 0:1], 0.0, ["rmask"])
        o2 = DEPTH * 8 + DEPTH * 4
        lbl = self.g2_sb[:, o2:o2 + 32].rearrange("p (h l) -> p h l", l=4)
        self.ACT(self.lbtmp[:], lbl, AF.Exp, ["g2_sb"], ["lbtmp"])
        self.p.add("dve", lambda e: e.tensor_reduce(out=self.lbs[:], in_=self.lbtmp[:], axis=mybir.AxisListType.X, op=ALU.add), ["lbtmp"], ["lbs"])
        self.p.add("dve", lambda e: e.reciprocal(out=self.lbs[:], in_=self.lbs[:]), ["lbs"], ["lbs"])
        self.TT(self.lbtmp[:], self.lbtmp[:], self.lbs[:].unsqueeze(2).to_broadcast([128, 8, 4]), ALU.mult, ["lbtmp", "lbs"], ["lbtmp"])
        self.MEMSET(self.lb[:, :, 0:1], 0.0, ["lb"])
        self.CP(self.lb[:, :, 1:2], self.lbtmp[:, :, 1:2], ["lbtmp", "lb"], ["lb"])
        self.TT(self.lb[:, :, 2:3], self.lb[:, :, 1:2], self.lbtmp[:, :, 2:3], ALU.add, ["lb", "lbtmp"], ["lb"])
        self.TT(self.lb[:, :, 3:4], self.lb[:, :, 2:3], self.lbtmp[:, :, 3:4], ALU.add, ["lb", "lbtmp"], ["lb"])
        self.TS(self.oml[:], self.lb[:], -1.0, 1.0, ALU.mult, ALU.add, ["lb"], ["oml"])

    def gcol(self, which, l):
        return {"ffn1": 0, "mix": 1, "ffn2": 2, "mem": 3}[which] * DEPTH * KC + l * KC

    def run_group(self, gi, g):
        P0, PL, NS = g
        self.P0, self.PL, self.NS = P0, PL, NS
        T = PL + 4 * NS
        self.T = T
        h = T // 2
        assert T % 2 == 0 and h <= HT and T <= TMAX and PL % CH == 0
        self.tiles = [(0, h), (h, T)]
        self.first = (P0 == 0)
        self.last = (P0 + PL == 2048)
        on = lambda k: self.stages is None or k in self.stages
        self.on = on
        if on("loadx"):
            self.load_x()
        for l in range(self.depth):
            if on("memkv"):
                self.memkv(l, write_out=(gi == 0))
            if on("ffn1"):
                self.ffn(l, self.w1a, self.w3a, self.w2a, "ffn1")
            if on("mixer"):
                self.mixer(l)
            if on("ffn2"):
                self.ffn(l, self.w1b, self.w3b, self.w2b, "ffn2")
        if on("final"):
            self.final_out()

    def tok_blocks(self, dprompt, dsample):
        blocks = []
        t = 0
        while t < self.PL:
            n = min(128, self.PL - t)
            blocks.append((dprompt[self.P0 + t:self.P0 + t + n, :], t, n))
            t += n
        if self.NS:
            blocks.append((dsample[0:4 * self.NS, :], self.PL, 4 * self.NS))
        return blocks

    def load_x(self):
        for src, t0, n in self.tok_blocks(self.xp, self.xs):
            for hf in range(2):
                si = self.rotate("stg", 2)
                self.DMA(self.stg[0:n, si, :], src[:, hf * 1024:(hf + 1) * 1024], (), [("stg", si)])
                for q in range(2):
                    b = self.bank()
                    for j in range(4):
                        kk = q * 4 + j
                        self.TR(self.ps[b][:, j * 128:j * 128 + n], self.stg[0:n, si, kk * 128:(kk + 1) * 128], self.ident_f(n),
                                [("stg", si), "c_sb"], [("ps", b)])
                    kc0 = hf * 8 + q * 4
                    wk = []
                    for j in range(4):
                        wk += self.tk("x", kc0 + j, t0, t0 + n)
                    src_ps = self.ps[b][:].rearrange("p (j c) -> p j c", c=128)[:, :, 0:n]
                    self.CP(self.x[:, kc0:kc0 + 4, t0:t0 + n], src_ps, [("ps", b)], wk, eng=("act" if q % 2 else "dve"))

    def rstd_tile(self, ti, c0, c1, dst, dkey, scale_n):
        n = c1 - c0
        b = self.bank()
        for kc in range(KC):
            j = self.rotate("sq", 2)
            self.ACT(self.sq[:, j, 0:n], self.x[:, kc, c0:c1], AF.Square, [("x", kc, ti)], [("sq", j)])
            self.MM(self.ps[b][:, 0:n], self.onesb[:], self.sq[:, j, 0:n], kc == 0, kc == KC - 1,
                    [("sq", j), "onesb"], [("ps", b)])
        self.ACT(dst, self.ps[b][:, 0:n], AF.Ln, [("ps", b), "epsc"], [dkey], bias=self.epsc[:], scale=1.0 / scale_n)
        self.ACT(dst, dst, AF.Exp, [dkey], [dkey], scale=-0.5)

    def rmsnorm(self, gbase):
        for ti, (c0, c1) in enumerate(self.tiles):
            n = c1 - c0
            r = self.rotate("rs", 2)
            self.rstd_tile(ti, c0, c1, self.rs[:, r, 0:n], ("rs", r), D)
            for kc in range(KC):
                self.STT(self.xn[:, kc, c0:c1], self.x[:, kc, c0:c1], self.g_sb[:, gbase + kc:gbase + kc + 1], self.rs[:, r, 0:n],
                         ALU.mult, ALU.mult, [("x", kc, ti), ("rs", r), "g_sb"], [("xn", kc, ti)])

    def ffn(self, l, w1, w3, w2, which):
        self.rmsnorm(self.gcol(which, l))
        parts = [(0, 16), (16, 32), (32, 44)]
        for (f0, f1) in parts:
            for f in range(f0, f1):
                s1 = self.wload(w1[l, f], D)
                s3 = self.wload(w3[l, f], D)
                W1 = self.wring[:, s1, :].rearrange("p (k m) -> p k m", m=128)
                W3 = self.wring[:, s3, :].rearrange("p (k m) -> p k m", m=128)
                for ti, (c0, c1) in enumerate(self.tiles):
                    n = c1 - c0
                    ba = self.bank()
                    bb = self.bank()
                    for kc in range(KC):
                        self.MM(self.ps[ba][:, 0:n], W1[:, kc, :], self.xn[:, kc, c0:c1], kc == 0, kc == KC - 1,
                                [("w", s1), ("xn", kc, ti)], [("ps", ba)])
                    for kc in range(KC):
                        self.MM(self.ps[bb][:, 0:n], W3[:, kc, :], self.xn[:, kc, c0:c1], kc == 0, kc == KC - 1,
                                [("w", s3), ("xn", kc, ti)], [("ps", bb)])
                    j = self.rotate("sa", 2)
                    self.ACT(self.sa[:, j, 0:n], self.ps[ba][:, 0:n], AF.Silu, [("ps", ba)], [("sa", j)])
                    self.TT(self.hy[:, f - f0, c0:c1], self.sa[:, j, 0:n], self.ps[bb][:, 0:n], ALU.mult,
                            [("sa", j), ("ps", bb)], [("hy", f - f0, ti)])
            nf = f1 - f0
            for oc in range(KC):
                s2 = self.wload(w2[l, oc][:, f0 * 128:f1 * 128], nf * 128)
                W2 = self.wring[:, s2, 0:nf * 128].rearrange("p (k m) -> p k m", m=128)
                for ti, (c0, c1) in enumerate(self.tiles):
                    n = c1 - c0
                    b = self.bank()
                    for j in range(nf):
                        self.MM(self.ps[b][:, 0:n], W2[:, j, :], self.hy[:, j, c0:c1], j == 0, j == nf - 1,
                                [("w", s2), ("hy", j, ti)], [("ps", b)])
                    self.STT(self.x[:, oc, c0:c1], self.ps[b][:, 0:n], 0.5, self.x[:, oc, c0:c1], ALU.mult, ALU.add,
                             [("ps", b), ("x", oc, ti)], [("x", oc, ti)])

    def memkv(self, l, write_out):
        gb = self.gcol("mem", l)
        for blk in range(2):
            for hf in range(2):
                self.DMA(self.stg[:, hf, :], self.memp[blk * 128:(blk + 1) * 128, hf * 1024:(hf + 1) * 1024], (), [("stg", hf)])
                for qq in range(2):
                    col = blk * 4 + hf * 2 + qq
                    self.p.add("act", lambda e, hf=hf, qq=qq, col=col: e.activation(
                        out=self.sf[4][:, 0:512], in_=self.stg[:, hf, qq * 512:(qq + 1) * 512], func=AF.Square,
                        accum_out=self.ss1[:, col:col + 1]), [("stg", hf)], [("ss1", blk), ("sf", 4)])
            self.p.add("dve", lambda e, blk=blk: e.tensor_reduce(out=self.ss2[:, blk:blk + 1], in_=self.ss1[:, blk * 4:blk * 4 + 4],
                                                                  axis=mybir.AxisListType.X, op=ALU.add), [("ss1", blk)], [("ss2", blk)])
            self.ACT(self.ss2[:, blk:blk + 1], self.ss2[:, blk:blk + 1], AF.Ln, [("ss2", blk), "epsc"], [("ss2", blk)], bias=self.epsc[:], scale=1.0 / D)
            self.ACT(self.ss2[:, blk:blk + 1], self.ss2[:, blk:blk + 1], AF.Exp, [("ss2", blk)], [("ss2", blk)], scale=-0.5)
            for hf in range(2):
                self.TS(self.stg[:, hf, :], self.stg[:, hf, :], self.ss2[:, blk:blk + 1], None, ALU.mult, None,
                        [("stg", hf), ("ss2", blk)], [("stg", hf)])
                for q in range(2):
                    b = self.bank()
                    for j in range(4):
                        kk = q * 4 + j
                        self.TR(self.ps[b][:, j * 128:(j + 1) * 128], self.stg[:, hf, kk * 128:(kk + 1) * 128], self.ident_f(128),
                                [("stg", hf), "c_sb"], [("ps", b)])
                    for j in range(4):
                        kc = hf * 8 + q * 4 + j
                        self.TS(self.obr[:, kc, blk * 128:(blk + 1) * 128], self.ps[b][:, j * 128:(j + 1) * 128],
                                self.g_sb[:, gb + kc:gb + kc + 1], None, ALU.mult, None, [("ps", b), "g_sb"], [("obr", kc, 0)])
        memT = lambda kc, a, b_: self.obr[:, kc, a:b_]
        if not self.on("mk_b"):
            return
        for wsrc, isk in ((self.wmk, True), (self.wmv, False)):
            oi = 0 if isk else 1
            for hh in range(4):
                s = self.wload(wsrc[l, hh], D)
                W = self.wring[:, s, :].rearrange("p (k m) -> p k m", m=128)
                if not self.on("mk_mm"):
                    continue
                if isk:
                    b = self.bank()
                    for kc in range(KC):
                        self.MM(self.ps[b][:, 0:256], W[:, kc, :], memT(kc, 0, 256), kc == 0, kc == KC - 1,
                                [("w", s), ("obr", kc, 0)], [("ps", b)])
                    self.CP(self.mkT[:, hh, :], self.ps[b][:, 0:256], [("ps", b)], ["mkT"], eng="act")
                if (not isk) or write_out:
                    b = self.bank()
                    for blk in range(2):
                        for kc in range(KC):
                            self.MM(self.ps[b][:, blk * 128:(blk + 1) * 128], memT(kc, blk * 128, (blk + 1) * 128), W[:, kc, :],
                                    kc == 0, kc == KC - 1, [("w", s), ("obr", kc, 0)], [("ps", b)])
                    src = self.ps[b][:, 0:256].rearrange("p (a c) -> p a c", c=128)
                    if not isk:
                        self.CP(self.mvt[:, :, hh * 128:(hh + 1) * 128], src, [("ps", b)], ["mvt"], eng="act")
                    if write_out:
                        self.CP(self.stg[:, oi, :].rearrange("p (a c) -> p a c", c=512)[:, :, hh * 128:(hh + 1) * 128], src,
                                [("ps", b)], [("stg", oi)], eng="dve")
            if write_out and self.on("mk_o"):
                dst = (self.mk_out if isk else self.mv_out)[l].rearrange("(a p) c -> p a c", p=128)
                self.DMA(dst, self.stg[:, oi, :].rearrange("p (a c) -> p a c", c=512), [("stg", oi)], [("mko", l, oi)])

    def mixer(self, l):
        on = self.on
        self.rmsnorm(self.gcol("mix", l))
        if on("hgrn"):
            for _ in self.hgrn_A(l, 0):
                pass
            for hd in range(NH):
                g = self.hgrn_A(l, hd + 1) if hd + 1 < NH else iter(())
                self.hgrn_B(l, hd, lambda g=g: next(g, None))
                for _ in g:
                    pass
        if on("pool"):
            self.pool_branch(l)
        if on("xattn"):
            self.xattn(l)
        if on("merge"):
            self.merge(l)

    def proj_fm(self, l, col_tile, evac):
        s = self.wload(self.win[l, col_tile], D)
        W = self.wring[:, s, :].rearrange("p (k m) -> p k m", m=128)
        for ti, (c0, c1) in enumerate(self.tiles):
            n = c1 - c0
            b = self.bank()
            for kc in range(KC):
                self.MM(self.ps[b][:, 0:n], W[:, kc, :], self.xn[:, kc, c0:c1], kc == 0, kc == KC - 1,
                        [("w", s), ("xn", kc, ti)], [("ps", b)])
            evac(ti, c0, c1, self.ps[b][:, 0:n], ("ps", b))

    def psO(self, a, b_):
        return self.ps[4][:, a:b_] if a < 512 else self.ps[5][:, a - 512:b_ - 512]

    def pkO(self, a):
        return ("ps", 4) if a < 512 else ("ps", 5)

    def hbuf(self, par, i):
        if par == 0:
            return self.sbb[i], [("sbb", i)]
        return self.hy[:, i, :], [("hy", i, 0), ("hy", i, 1)]

    def ogbuf(self, par):
        j = 4 + 2 * par
        ap = self.hy[:, j:j + 2, :].rearrange("p a t -> p (a t)").bitcast(F32)
        return ap, [("hy", j, 0), ("hy", j, 1), ("hy", j + 1, 0), ("hy", j + 1, 1)]

    def proj_g(self, l, col_tile, evac):
        s = self.wload(self.win[l, col_tile], D)
        W = self.wring[:, s, :].rearrange("p (k m) -> p k m", m=128)
        for ti, (c0, c1) in enumerate(self.tiles):
            n = c1 - c0
            b = self.bank()
            for kc in range(KC):
                self.MM(self.ps[b][:, 0:n], W[:, kc, :], self.xn[:, kc, c0:c1], kc == 0, kc == KC - 1,
                        [("w", s), ("xn", kc, ti)], [("ps", b)])
            evac(ti, c0, c1, self.ps[b][:, 0:n], ("ps", b))
            yield

    def hgrn_A(self, l, hd):
        PL, NS, T = self.PL, self.NS, self.T
        par = hd % 2
        sf = self.sf
        K = lambda i: ("sf", i)
        (qt, qk), (kt, kk), (kh, khk), (vv, vk) = [self.hbuf(par, i) for i in range(4)]
        og, ogk = self.ogbuf(par)
        dd = self.dd[:, par, :]
        ddk = ("dd", par)
        yield from self.proj_g(l, hd, lambda ti, c0, c1, ps, pk: self.ACT(sf[0][:, c0:c1], ps, AF.Silu, [pk], [K(0)]))
        yield from self.proj_g(l, 24 + hd, lambda ti, c0, c1, ps, pk: self.ACT(og[:, c0:c1], ps, AF.Silu, [pk], ogk))
        yield from self.proj_g(l, 8 + hd, lambda ti, c0, c1, ps, pk: self.ACT(sf[1][:, c0:c1], ps, AF.Sigmoid, [pk], [K(1)]))
        yield from self.proj_g(l, 16 + hd, lambda ti, c0, c1, ps, pk: self.CP(vv[:, c0:c1], ps, [pk], vk, eng="dve"))
        lbc = self.lb[:, hd, l:l + 1]
        omc = self.oml[:, hd, l:l + 1]
        self.TS(sf[1][:, 0:T], sf[1][:, 0:T], omc, lbc, ALU.mult, ALU.add, [K(1), "oml", "lb"], [K(1)])
        yield
        self.TS(sf[2][:, 0:T], sf[1][:, 0:T], -1.0, 1.0, ALU.mult, ALU.add, [K(1)], [K(2)])
        self.ACT(sf[3][:, 0:T], sf[1][:, 0:T], AF.Ln, [K(1)], [K(3)])
        yield
        self.p.add("dve", lambda e: e.tensor_tensor_scan(out=sf[4][:, 0:T], data0=self.rmask[:, 704 - PL:704 - PL + T], data1=sf[3][:, 0:T],
                                                         initial=0.0, op0=ALU.mult, op1=ALU.add), [K(3), "rmask"], [K(4)])
        yield
        self.ACT(sf[3][:, 0:T], sf[4][:, 0:T], AF.Exp, [K(4)], [K(3)])
        yield
        self.TT(qt[:, 0:T], sf[0][:, 0:T], sf[3][:, 0:T], ALU.mult, [K(0), K(3)], qk)
        self.ACT(sf[1][:, 0:T], sf[4][:, 0:T], AF.Exp, [K(4)], [K(1)], scale=-1.0)
        yield
        self.TT(kt[:, 0:T], sf[2][:, 0:T], sf[1][:, 0:T], ALU.mult, [K(2), K(1)], kk)
        yield
        nch = PL // CH
        b3 = sf[4][:, 0:PL].rearrange("p (c j) -> p c j", j=CH)
        self.ACT(dd[:, 0:nch], sf[4][:, CH - 1:PL:CH], AF.Exp, [K(4)], [ddk])
        self.TT(sf[3][:, 0:PL].rearrange("p (c j) -> p c j", j=CH), b3[:, :, CH - 1:CH].to_broadcast([128, nch, CH]), b3, ALU.subtract,
                [K(4), K(3)], [K(3)])
        yield
        if NS:
            b4 = sf[4][:, PL:T].rearrange("p (c j) -> p c j", j=4)
            self.ACT(dd[:, 24:24 + NS], sf[4][:, PL + 3:T:4], AF.Exp, [K(4)], [ddk])
            self.TT(sf[3][:, PL:T].rearrange("p (c j) -> p c j", j=4), b4[:, :, 3:4].to_broadcast([128, NS, 4]), b4, ALU.subtract,
                    [K(4), K(3)], [K(3)])
            yield
        self.ACT(sf[3][:, 0:T], sf[3][:, 0:T], AF.Exp, [K(3)], [K(3)])
        yield
        self.TT(kh[:, 0:T], sf[2][:, 0:T], sf[3][:, 0:T], ALU.mult, [K(2), K(3)], khk)
        yield

    def hgrn_B(self, l, hd, step):
        PL, NS, T = self.PL, self.NS, self.T
        par = hd % 2
        psO, pkO = self.psO, self.pkO
        (qt, qk), (kt, kk), (kh, khk), (vv, vk) = [self.hbuf(par, i) for i in range(4)]
        og, ogk = self.ogbuf(par)
        dd = self.dd[:, par, :]
        ddk = ("dd", par)
        nch = PL // CH
        psT = self.ps[6][:].bitcast(BF16)
        if self.first:
            self.MEMSET(self.Sf[:, 0, :], 0.0, [("Sf", 0)])
            self.MEMSET(self.Sb[:, 0, :], 0.0, [("Sb", 0)])
        else:
            self.DMA(self.Sf[:, 0, :], self.handS[l, hd], [("handS", l, hd)], [("Sf", 0)])
            self.CP(self.Sb[:, 0, :], self.Sf[:, 0, :], [("Sf", 0)], [("Sb", 0)], eng="act")
        cmask = self.c_sb[0:CH, 128:128 + CH]
        cur = 0
        for c in range(nch):
            sl = (c // 4) % 2
            cj = c % 4
            if cj == 0:
                m = min(4, nch - c)
                for (src, dst, dk, bk, off) in ((kh, self.tokk, "tokk", khk, 0), (vv, self.tokv, "tokv", vk, 512)):
                    for j in range(m):
                        self.TR(psT[0:CH, off + j * 128:off + (j + 1) * 128], src[:, (c + j) * CH:(c + j + 1) * CH], self.identb[:],
                                bk + ["identb"], [("ps", 6)])
                    self.CP(dst[0:CH, sl, 0:m, :], psT[0:CH, off:off + m * 128].rearrange("p (a c) -> p a c", c=128), [("ps", 6)], [(dk, sl)],
                            eng=("act" if off else "dve"))
            a0, a1 = c * CH, (c + 1) * CH
            aj = self.rotate("atsb", 2)
            nxt = (cur + 1) % 4
            fc, fn = c % 2, (c + 1) % 2
            bk = self.bank()
            self.MM(self.ps[7][0:CH, 0:CH], kt[:, a0:a1], qt[:, a0:a1], True, True, kk + qk, [("ps", 7)])
            self.MM(self.ps[bk][:, 0:128], self.tokk[0:CH, sl, cj, :], self.tokv[0:CH, sl, cj, :], True, True, [("tokk", sl), ("tokv", sl)], [("ps", bk)])
            self.TT(self.atsb[0:CH, aj, 0:CH], self.ps[7][0:CH, 0:CH], cmask, ALU.mult, [("ps", 7), "c_sb"], [("atsb", aj)])
            self.STT(self.Sf[:, fn, :], self.Sf[:, fc, :], dd[:, c:c + 1], self.ps[bk][:, 0:128], ALU.mult, ALU.add,
                     [("Sf", fc), ddk, ("ps", bk)], [("Sf", fn)])
            self.CP(self.Sb[:, nxt, :], self.Sf[:, fn, :], [("Sf", fn)], [("Sb", nxt)], eng="act")
            self.MM(psO(a0, a1), self.tokv[0:CH, sl, cj, :], self.atsb[0:CH, aj, 0:CH], True, False, [("tokv", sl), ("atsb", aj)], [pkO(a0)])
            self.MM(psO(a0, a1), self.Sb[:, cur, :], qt[:, a0:a1], False, True, [("Sb", cur)] + qk, [pkO(a0)])
            cur = nxt
            step()
        fcur = nch % 2
        if self.last:
            self.DMA(self.hsp[l, hd], self.Sf[:, fcur, :], [("Sf", fcur)], [("hsp", l, hd)])
        else:
            self.DMA(self.handS[l, hd], self.Sf[:, fcur, :], [("Sf", fcur)], [("handS", l, hd)])
        if NS:
            n4 = 4 * NS
            smask = self.c_sb[0:n4, 160:160 + n4]
            for (src, dst, dk, bk) in ((kh, self.tokks, "tokks", khk), (vv, self.tokvs, "tokvs", vk)):
                self.TR(psT[0:n4, 0:128], src[:, PL:T], self.identb[:], bk + ["identb"], [("ps", 6)])
                self.CP(dst[0:n4, :], psT[0:n4, 0:128], [("ps", 6)], [dk], eng="act")
            aj = self.rotate("atsb", 2)
            self.MM(self.ps[7][0:n4, 0:n4], kt[:, PL:T], qt[:, PL:T], True, True, kk + qk, [("ps", 7)])
            self.TT(self.atsb[0:n4, aj, 0:n4], self.ps[7][0:n4, 0:n4], smask, ALU.mult, [("ps", 7), "c_sb"], [("atsb", aj)])
            self.MM(psO(PL, T), self.tokvs[0:n4, :], self.atsb[0:n4, aj, 0:n4], True, False, ["tokvs", ("atsb", aj)], [pkO(PL)])
            for bt in range(NS // 4):
                sl = self.rotate("S0", 2)
                self.DMA(self.S0f[:, sl, :, :], self.hs_in[l, bt * 4:bt * 4 + 4, hd].rearrange("s k v -> k s v"), (), [("S0f", sl)])
                self.CP(self.S0b[:, sl, :, :], self.S0f[:, sl, :, :], [("S0f", sl)], [("S0b", sl)], eng="act")
                for j in range(4):
                    s = bt * 4 + j
                    self.MM(psO(PL + 4 * s, PL + 4 * s + 4), self.S0b[:, sl, j, :], qt[:, PL + 4 * s:PL + 4 * s + 4], False, s == NS - 1,
                            [("S0b", sl)] + qk, [pkO(PL)])
                selm = self.c_sb[0:n4, 224 + bt * 4:224 + bt * 4 + 4]
                self.TT(self.kmask[0:n4, sl, :, :], self.tokks[0:n4, :].unsqueeze(1).to_broadcast([n4, 4, 128]),
                        selm.unsqueeze(2).to_broadcast([n4, 4, 128]), ALU.mult, ["tokks", "c_sb"], [("kmask", sl)])
                b = self.bank()
                for j in range(4):
                    self.MM(self.ps[b][:, j * 128:(j + 1) * 128], self.kmask[0:n4, sl, j, :], self.tokvs[0:n4, :], True, True,
                            [("kmask", sl), "tokvs"], [("ps", b)])
                for j in range(4):
                    s = bt * 4 + j
                    self.STT(self.S0f[:, sl, j, :], self.S0f[:, sl, j, :], dd[:, 24 + s:25 + s], self.ps[b][:, j * 128:(j + 1) * 128],
                             ALU.mult, ALU.add, [("S0f", sl), ddk, ("ps", b)], [("S0f", sl)])
                self.DMA(self.hss[l, bt * 4:bt * 4 + 4, hd].rearrange("s k v -> k s v"), self.S0f[:, sl, :, :], [("S0f", sl)], [("hss", l, hd, bt)])
                step()
        hgc = self.g2_sb[:, l * 8 + hd:l * 8 + hd + 1]
        for ti, (c0, c1) in enumerate(self.tiles):
            segs = []
            a = c0
            while a < c1:
                e_ = min(c1, 512) if a < 512 else c1
                segs.append((a, e_))
                a = e_
            for (a, e_) in segs:
                n = e_ - a
                j = self.rotate("sq", 2)
                self.ACT(self.sq[:, j, 0:n], psO(a, e_), AF.Square, [pkO(a)], [("sq", j)])
                b = self.bank()
                self.MM(self.ps[b][:, 0:n], self.onesb[:], self.sq[:, j, 0:n], True, True, [("sq", j), "onesb"], [("ps", b)])
                r = self.rotate("rs", 2)
                self.ACT(self.rs[:, r, 0:n], self.ps[b][:, 0:n], AF.Ln, [("ps", b), "epsc"], [("rs", r)], bias=self.epsc[:], scale=1.0 / 128)
                self.ACT(self.rs[:, r, 0:n], self.rs[:, r, 0:n], AF.Exp, [("rs", r)], [("rs", r)], scale=-0.5)
                self.STT(self.rs[:, r, 0:n], psO(a, e_), hgc, self.rs[:, r, 0:n], ALU.mult, ALU.mult, [pkO(a), ("rs", r), "g2_sb"], [("rs", r)])
                self.TT(self.obr[:, hd, a:e_], self.rs[:, r, 0:n], og[:, a:e_], ALU.mult, [("rs", r)] + ogk, self.tk("obr", hd, a, e_))
                step()

    def pool_branch(self, l):
        PL, NS, T = self.PL, self.NS, self.T
        sf = self.sf
        K = lambda i: ("sf", i)
        if NS:
            for half in range(2):
                self.DMA(self.stg[0:120, half, 0:512], self.pool_in[l, half * 120:(half + 1) * 120, :], (), [("stg", half)])
        for g in range(4):
            w = POOL_W[g]
            pscol = self.g2_sb[:, DEPTH * 8 + l * 4 + g:DEPTH * 8 + l * 4 + g + 1]

            def ev(ti, c0, c1, ps, pk):
                p1 = min(c1, PL)
                if c0 < p1:
                    self.CP(sf[0][:, 15 + c0:15 + p1], ps[:, 0:p1 - c0], [pk], [K(0)], eng="act")
                if NS and c1 > PL:
                    s0 = max(c0, PL)
                    self.CP(self.uus[:, (s0 - PL) // 4:(c1 - PL) // 4, 15:19],
                            ps[:, s0 - c0:c1 - c0].rearrange("p (s j) -> p s j", j=4), [pk], ["uus"], eng="act")
            if NS:
                b = self.bank()
                for half in range(2):
                    self.TR(self.ps[b][:, half * 128:half * 128 + 120], self.stg[0:120, half, g * 128:(g + 1) * 128], self.ident_f(120),
                            [("stg", half), "c_sb"], [("ps", b)])
                for half in range(2):
                    self.CP(self.uus[:, half * 8:(half + 1) * 8, 0:15],
                            self.ps[b][:, half * 128:half * 128 + 120].rearrange("p (s r) -> p s r", r=15),
                            [("ps", b)], ["uus"], eng="dve")
            if self.first:
                self.MEMSET(sf[0][:, 0:15], 0.0, [K(0)])
            else:
                self.CP(sf[0][:, 0:15], self.ptail[:, l, g, 0:15], ["ptail"], [K(0)])
            self.proj_fm(l, 32 + g, ev)
            NP = 15 + PL
            bufs = [0, 1, 2, 1, 2]
            step = 1
            cur = 0
            for it in range(g + 1):
                nxt = bufs[it + 1]
                self.TT(sf[nxt][:, step:NP], sf[cur][:, step:NP], sf[cur][:, 0:NP - step], ALU.add, [K(cur)], [K(nxt)])
                cur = nxt
                step *= 2
            self.STT(self.sbb[0][:, 0:PL], sf[cur][:, 15:NP], 1.0 / w, sf[0][:, 15:NP], ALU.mult, ALU.subtract, [K(cur), K(0)], [("sbb", 0)])
            if self.first:
                ic = self.c_sb[:, 240 + g * 16:240 + g * 16 + 16]
                self.TT(sf[3][:, 0:16], sf[cur][:, 15:31], ic, ALU.mult, [K(cur), "c_sb"], [K(3)])
                self.TT(self.sbb[0][:, 0:16], sf[3][:, 0:16], sf[0][:, 15:31], ALU.subtract, [K(3), K(0)], [("sbb", 0)])
            if self.last:
                b = self.bank()
                self.TR(self.ps[b][0:32, 0:128], sf[0][:, 15 + PL - 32:15 + PL], self.ident_f(128), [K(0), "c_sb"], [("ps", b)])
                self.CP(self.stg[0:32, 0, 512 + g * 128:512 + (g + 1) * 128], self.ps[b][0:32, 0:128], [("ps", b)], [("stg", 0)], eng="dve")
            else:
                self.CP(self.ptail[:, l, g, 0:15], sf[0][:, PL:PL + 15], [K(0)], ["ptail"])
            sW = self.wload(self.wpool[l, g], 128)
            for ti, (c0, c1) in enumerate(self.tiles):
                p1 = min(c1, PL)
                if c0 >= p1:
                    continue
                n = p1 - c0
                b = self.bank()
                self.MM(self.ps[b][:, 0:n], self.wring[:, sW, 0:128], self.sbb[0][:, c0:p1], True, True, [("w", sW), ("sbb", 0)], [("ps", b)])
                self.TS(self.obr[:, 8 + g, c0:p1], self.ps[b][:, 0:n], pscol, None, ALU.mult, None, [("ps", b), "g2_sb"],
                        self.tk("obr", 8 + g, c0, p1))
            if NS:
                self.pool_sample(l, g, sW, pscol)
        if self.last:
            self.DMA(self.ppool[l], self.stg[17:32, 0, 512:1024], [("stg", 0)], [("ppool", l)])
        if NS:
            self.DMA(self.pools[l, :, 0:11, :], self.pool_in[l].rearrange("(s r) c -> s r c", r=15)[:, 4:15, :], (), [("pools_a", l)])
            for j in range(4):
                self.DMA(self.pools[l, :, 11 + j, :], self.stg[j:4 * NS:4, 1, 512:1024], [("stg", 1)], [("pools_b", l, j)])

    def pool_sample(self, l, g, sW, pscol):
        PL, NS, T = self.PL, self.NS, self.T
        w = POOL_W[g]
        t1 = self.uu2
        t2 = self.sf[3][:, 0:NS * 19].rearrange("p (s r) -> p s r", r=19)
        seq = [self.uus, t1, t2, t1, t2]
        keys = ["uus", "uu2", ("sf", 3), "uu2", ("sf", 3)]
        step = 1
        cur = 0
        for it in range(g + 1):
            nx = it + 1
            self.TT(seq[nx][:, :, step:19], seq[cur][:, :, step:19], seq[cur][:, :, 0:19 - step], ALU.add, [keys[cur]], [keys[nx]])
            cur = nx
            step *= 2
        n = 4 * NS
        dst = self.sbb[1][:, 0:n].rearrange("p (s j) -> p s j", j=4)
        self.STT(dst, seq[cur][:, :, 15:19], 1.0 / w, self.uus[:, :, 15:19], ALU.mult, ALU.subtract, [keys[cur], "uus"], [("sbb", 1)])
        b = self.bank()
        self.MM(self.ps[b][:, 0:n], self.wring[:, sW, 0:128], self.sbb[1][:, 0:n], True, True, [("w", sW), ("sbb", 1)], [("ps", b)])
        self.TS(self.obr[:, 8 + g, PL:T], self.ps[b][:, 0:n], pscol, None, ALU.mult, None, [("ps", b), "g2_sb"], self.tk("obr", 8 + g, PL, T))
        j = self.rotate("rs", 2)
        self.CP(self.rs[:, j, 0:n].rearrange("p (s j) -> p s j", j=4), self.uus[:, :, 15:19], ["uus"], [("rs", j)])
        b = self.bank()
        self.TR(self.ps[b][0:n, 0:128], self.rs[:, j, 0:n], self.ident_f(128), [("rs", j), "c_sb"], [("ps", b)])
        self.CP(self.stg[0:n, 1, 512 + g * 128:512 + (g + 1) * 128], self.ps[b][0:n, 0:128], [("ps", b)], [("stg", 1)], eng="dve")

    def xattn(self, l):
        PL, NS, T = self.PL, self.NS, self.T
        sc = 128 ** -0.5
        for hh in range(4):
            self.proj_fm(l, 36 + hh, lambda ti, c0, c1, ps, pk: self.ACT(self.sbb[0][:, c0:c1], ps, AF.Copy, [pk], [("sbb", 0)], scale=sc))
            if NS:
                self.CP(self.sbb[3][:, hh * 64:hh * 64 + 4 * NS], self.sbb[0][:, PL:T], [("sbb", 0)], [("sbb", 3)])
            for ti, (c0, c1) in enumerate(self.tiles):
                p1 = min(c1, PL)
                if c0 >= p1:
                    continue
                n = p1 - c0
                b0 = self.bank()
                b1 = self.bank()
                for mc, b in ((0, b0), (1, b1)):
                    self.MM(self.ps[b][:, 0:n], self.mkT[:, hh, mc * 128:(mc + 1) * 128], self.sbb[0][:, c0:p1], True, True,
                            ["mkT", ("sbb", 0)], [("ps", b)])
                e0 = self.sbb[1]
                e1 = self.sbb[2]
                self.ACT(e0[:, 0:n], self.ps[b0][:, 0:n], AF.Exp, [("ps", b0)], [("sbb", 1)])
                self.ACT(e1[:, 0:n], self.ps[b1][:, 0:n], AF.Exp, [("ps", b1)], [("sbb", 2)])
                bs = self.bank()
                self.MM(self.ps[bs][:, 0:n], self.onesb[:], e0[:, 0:n], True, False, ["onesb", ("sbb", 1)], [("ps", bs)])
                self.MM(self.ps[bs][:, 0:n], self.onesb[:], e1[:, 0:n], False, True, ["onesb", ("sbb", 2)], [("ps", bs)])
                bo = self.bank()
                self.MM(self.ps[bo][:, 0:n], self.mvt[:, 0, hh * 128:(hh + 1) * 128], e0[:, 0:n], True, False, ["mvt", ("sbb", 1)], [("ps", bo)])
                self.MM(self.ps[bo][:, 0:n], self.mvt[:, 1, hh * 128:(hh + 1) * 128], e1[:, 0:n], False, True, ["mvt", ("sbb", 2)], [("ps", bo)])
                r = self.rotate("rs", 2)
                self.p.add("dve", lambda e, r=r, bs=bs, n=n: e.reciprocal(out=self.rs[:, r, 0:n], in_=self.ps[bs][:, 0:n]), [("ps", bs)], [("rs", r)])
                self.TT(self.obr[:, 12 + hh, c0:p1], self.ps[bo][:, 0:n], self.rs[:, r, 0:n], ALU.mult, [("ps", bo), ("rs", r)],
                        self.tk("obr", 12 + hh, c0, p1))
        if NS:
            self.xattn_sample(l)

    def xattn_sample(self, l):
        PL, NS, T = self.PL, self.NS, self.T
        psS = self.ps[6]
        psS5 = psS[:].rearrange("p (s h c t) -> p s h c t", s=NS, h=4, c=2)
        psT = self.ps[7][:].bitcast(BF16)
        e5 = self.sbb[1][:, 0:NS * 32].rearrange("p (s h c t) -> p s h c t", s=NS, h=4, c=2)
        for s in range(NS):
            j = self.rotate("ckb", 2)
            self.DMA(self.ckb[:, j, :, :], self.ck_in[l, s].rearrange("(a p) c -> p a c", p=128), (), [("ckb", j)], q="pool", stream="ck%d" % j)
            self.DMA(self.cvb[:, j, :, :], self.cv_in[l, s].rearrange("(a p) c -> p a c", p=128), (), [("cvb", j)], q="pool", stream="cv%d" % j)
            for hh in range(4):
                for mc in range(2):
                    self.TR(psT[:, (hh * 2 + mc) * 128:(hh * 2 + mc + 1) * 128], self.ckb[:, j, mc, hh * 128:(hh + 1) * 128], self.identb[:],
                            [("ckb", j), "identb"], [("ps", 7)])
            self.CP(self.mkTs[:, :, :], psT[:, 0:1024].rearrange("p (h m) -> p h m", m=256), [("ps", 7)], ["mkTs"], eng=("act" if s % 2 else "dve"))
            for hh in range(4):
                for mc in range(2):
                    self.MM(psS5[:, s, hh, mc, :], self.mkTs[:, hh, mc * 128:(mc + 1) * 128],
                            self.sbb[3][:, hh * 64 + 4 * s:hh * 64 + 4 * s + 4], True, True, ["mkTs", ("sbb", 3)], [("ps", 6)])
            self.ACT(self.sbb[1][:, s * 32:(s + 1) * 32], psS[:, s * 32:(s + 1) * 32], AF.Exp, [("ps", 6)], [("sbb", 1)])
            for hh in range(4):
                for mc in range(2):
                    self.MM(self.ps[5][:, 256 + s * 16 + hh * 4:256 + s * 16 + hh * 4 + 4], self.cvb[:, j, mc, hh * 128:(hh + 1) * 128],
                            e5[:, s, hh, mc, :], mc == 0, mc == 1, [("cvb", j), ("sbb", 1)], [("ps", 5)])
        b = self.bank()
        self.CP(self.sbb[2][:, 0:NS * 16].rearrange("p (s h t) -> p s h t", s=NS, h=4), e5[:, :, :, 0, :], [("sbb", 1)], [("sbb", 2)])
        self.CP(self.sbb[2][:, 256:256 + NS * 16].rearrange("p (s h t) -> p s h t", s=NS, h=4), e5[:, :, :, 1, :], [("sbb", 1)], [("sbb", 2)])
        self.MM(self.ps[b][:, 0:NS * 16], self.onesb[:], self.sbb[2][:, 0:NS * 16], True, False, ["onesb", ("sbb", 2)], [("ps", b)])
        self.MM(self.ps[b][:, 0:NS * 16], self.onesb[:], self.sbb[2][:, 256:256 + NS * 16], False, True, ["onesb", ("sbb", 2)], [("ps", b)])
        r = self.rotate("rs", 2)
        self.p.add("dve", lambda e: e.reciprocal(out=self.rs[:, r, 0:NS * 16], in_=self.ps[b][:, 0:NS * 16]), [("ps", b)], [("rs", r)])
        for hh in range(4):
            src = self.ps[5][:, 256:256 + NS * 16].rearrange("p (s h t) -> p s h t", s=NS, h=4)[:, :, hh, :]
            rr = self.rs[:, r, 0:NS * 16].rearrange("p (s h t) -> p s h t", s=NS, h=4)[:, :, hh, :]
            self.TT(self.obr[:, 12 + hh, PL:T].rearrange("p (s t) -> p s t", t=4), src, rr, ALU.mult, [("ps", 5), ("rs", r)],
                    self.tk("obr", 12 + hh, PL, T))

    def merge(self, l):
        segs = ((0, 8), (8, 12), (12, 16))
        acc = self.sf[0]
        tmp = self.sf[1]
        for oc in range(KC):
            for br in range(3):
                k0, k1 = segs[br]
                nk = k1 - k0
                sG = self.wload(self.win[l, 40 + br * 16 + oc], D)
                sB = self.wload(self.wbr[l, oc][:, k0 * 128:k1 * 128], nk * 128)
                WG = self.wring[:, sG, :].rearrange("p (k m) -> p k m", m=128)
                WB = self.wring[:, sB, 0:nk * 128].rearrange("p (k m) -> p k m", m=128)
                for ti, (c0, c1) in enumerate(self.tiles):
                    n = c1 - c0
                    bg = self.bank()
                    for kc in range(KC):
                        self.MM(self.ps[bg][:, 0:n], WG[:, kc, :], self.xn[:, kc, c0:c1], kc == 0, kc == KC - 1,
                                [("w", sG), ("xn", kc, ti)], [("ps", bg)])
                    bp = self.bank()
                    for kk in range(nk):
                        self.MM(self.ps[bp][:, 0:n], WB[:, kk, :], self.obr[:, k0 + kk, c0:c1], kk == 0, kk == nk - 1,
                                [("w", sB), ("obr", k0 + kk, ti)], [("ps", bp)])
                    j = self.rotate("sa", 2)
                    self.ACT(self.sa[:, j, 0:n], self.ps[bg][:, 0:n], AF.Sigmoid, [("ps", bg)], [("sa", j)])
                    if br == 0:
                        self.TT(acc[:, c0:c1], self.sa[:, j, 0:n], self.ps[bp][:, 0:n], ALU.mult, [("sa", j), ("ps", bp)], [("sf", 0)])
                    else:
                        self.TT(tmp[:, c0:c1], self.sa[:, j, 0:n], self.ps[bp][:, 0:n], ALU.mult, [("sa", j), ("ps", bp)], [("sf", 1)])
                        if br == 1:
                            self.TT(acc[:, c0:c1], acc[:, c0:c1], tmp[:, c0:c1], ALU.add, [("sf", 0), ("sf", 1)], [("sf", 0)])
                        else:
                            self.TT(self.hy[:, oc, c0:c1], acc[:, c0:c1], tmp[:, c0:c1], ALU.add, [("sf", 0), ("sf", 1)], [("hy", oc, ti)])
        for oc in range(KC):
            s = self.wload(self.wout[l, oc], D)
            W = self.wring[:, s, :].rearrange("p (k m) -> p k m", m=128)
            for ti, (c0, c1) in enumerate(self.tiles):
                n = c1 - c0
                b = self.bank()
                for kc in range(KC):
                    self.MM(self.ps[b][:, 0:n], W[:, kc, :], self.hy[:, kc, c0:c1], kc == 0, kc == KC - 1,
                            [("w", s), ("hy", kc, ti)], [("ps", b)])
                self.TT(self.x[:, oc, c0:c1], self.ps[b][:, 0:n], self.x[:, oc, c0:c1], ALU.add, [("ps", b), ("x", oc, ti)], [("x", oc, ti)])

    def final_out(self):
        PL, NS, T = self.PL, self.NS, self.T
        on = self.on
        gb = 4 * DEPTH * KC
        if on("f_rstd"):
            for ti, (c0, c1) in enumerate(self.tiles):
                self.rstd_tile(ti, c0, c1, self.sf[0][:, c0:c1], ("sf", 0), D)
        ynb = self.stg[:, 0, :].rearrange("p (k c) -> p k c", c=128)
        for bi, (dst, t0, n) in enumerate(self.tok_blocks(self.yp, self.ys)):
            for hf in range(2):
                if on("f_stt"):
                    for kk in range(8):
                        kc = hf * 8 + kk
                        self.STT(ynb[:, kk, 0:n], self.x[:, kc, t0:t0 + n], self.g_sb[:, gb + kc:gb + kc + 1], self.sf[0][:, t0:t0 + n],
                                 ALU.mult, ALU.mult, self.tk("x", kc, t0, t0 + n) + [("sf", 0), "g_sb"], [("stg", 0)])
                for q in range(2):
                    b = self.bank()
                    if on("f_tr"):
                        for j in range(4):
                            kk = q * 4 + j
                            self.TR(self.ps[b][0:n, j * 128:(j + 1) * 128], ynb[:, kk, 0:n], self.ident_f(128), [("stg", 0), "c_sb"], [("ps", b)])
                    if on("f_cp"):
                        self.CP(self.stg[0:n, 1, q * 512:(q + 1) * 512], self.ps[b][0:n, :], [("ps", b)], [("stg", 1)], eng=("act" if q % 2 else "dve"))
                if on("f_dma"):
                    self.DMA(dst[:, hf * 1024:(hf + 1) * 1024], self.stg[0:n, 1, :], [("stg", 1)], [("yout", self.P0, bi, hf)])


def _fm_tiles(w, kc):
    K, M = w.shape
    mc = M // 128
    return np.ascontiguousarray(w.reshape(kc, 128, mc, 128).transpose(2, 1, 0, 3)).reshape(mc, 128, kc * 128)


def _fm_tiles_L(w, kc):
    return np.stack([_fm_tiles(w[l], kc) for l in range(w.shape[0])])


def _vec_fm(v):
    lead = v.shape[:-1]
    n = v.shape[-1] // 128
    a = v.reshape(*lead, n, 128)
    return np.ascontiguousarray(np.moveaxis(a, -1, 0))


def _consts():
    c = np.zeros((128, CW), np.float32)
    c[:, 0:128] = np.eye(128, dtype=np.float32)
    j = np.arange(32)[:, None]
    t = np.arange(32)[None, :]
    c[0:32, 128:160] = (j <= t).astype(np.float32)
    j = np.arange(64)[:, None]
    t = np.arange(64)[None, :]
    c[0:64, 160:224] = ((j <= t) & (j // 4 == t // 4)).astype(np.float32)
    c[0:64, 224:240] = (np.arange(64)[:, None] // 4 == np.arange(16)[None, :]).astype(np.float32)
    for g, w in enumerate(POOL_W):
        c[:, 240 + g * 16:240 + g * 16 + 16] = 1.0 / np.minimum(w, np.arange(16) + 1.0)
    return c


_NC_CACHE = {}


def _get_nc(depth, groups):
    key = (depth, tuple(groups))
    if key not in _NC_CACHE:
        _NC_CACHE[key] = Builder(depth=depth, groups=groups).build()
    return _NC_CACHE[key]


def prepare_inputs(x_prompt, x_sample, state_hgrn, state_pool, cache_mem_k, cache_mem_v, mem_prompt,
                   ffn1_norm, ffn1_w1, ffn1_w3, ffn1_w2, mix_norm, w_in, lb_logits, hg_norm, w_pool,
                   pool_scale, mem_norm, w_mk, w_mv, w_branch, w_out, ffn2_norm, ffn2_w1, ffn2_w3,
                   ffn2_w2, final_norm, cores=range(8)):
    f = lambda a: np.asarray(a, dtype=np.float32)
    gv = np.concatenate([_vec_fm(f(ffn1_norm)).reshape(128, -1), _vec_fm(f(mix_norm)).reshape(128, -1),
                         _vec_fm(f(ffn2_norm)).reshape(128, -1), _vec_fm(f(mem_norm)).reshape(128, -1),
                         _vec_fm(f(final_norm)).reshape(128, -1)], axis=1)
    assert gv.shape == (128, GW)
    lbl = _vec_fm(f(lb_logits))
    lbl = np.ascontiguousarray(lbl.transpose(0, 2, 1)).reshape(128, 32)
    gv2 = np.concatenate([_vec_fm(f(hg_norm)).reshape(128, -1), _vec_fm(f(pool_scale)).reshape(128, -1), lbl], axis=1)
    assert gv2.shape == (128, G2W)
    shared = {
        "gvec": np.ascontiguousarray(gv), "gvec2": np.ascontiguousarray(gv2), "consts": _consts(),
        "w1a": _fm_tiles_L(f(ffn1_w1), KC), "w3a": _fm_tiles_L(f(ffn1_w3), KC), "w2a": _fm_tiles_L(f(ffn1_w2), FC),
        "w1b": _fm_tiles_L(f(ffn2_w1), KC), "w3b": _fm_tiles_L(f(ffn2_w3), KC), "w2b": _fm_tiles_L(f(ffn2_w2), FC),
        "win": _fm_tiles_L(f(w_in), KC), "wbr": _fm_tiles_L(f(w_branch), KC), "wout": _fm_tiles_L(f(w_out), KC),
        "wmk": _fm_tiles_L(f(w_mk), KC), "wmv": _fm_tiles_L(f(w_mv), KC),
        "wpool": np.ascontiguousarray(f(w_pool)),
    }
    xp = f(x_prompt)
    xs = f(x_sample)
    in_maps = []
    for c in cores:
        sq = c % 4
        s0 = 16 * c
        m = dict(shared)
        m["xp"] = np.ascontiguousarray(xp[sq])
        m["xs"] = np.ascontiguousarray(xs[s0:s0 + 16].reshape(64, D))
        m["memp"] = np.ascontiguousarray(f(mem_prompt)[sq])
        m["hs_in"] = np.ascontiguousarray(f(state_hgrn)[:, s0:s0 + 16])
        m["pool_in"] = np.ascontiguousarray(f(state_pool)[:, s0:s0 + 16].reshape(DEPTH, 16 * 15, 512))
        m["ck_in"] = np.ascontiguousarray(f(cache_mem_k)[:, s0:s0 + 16].reshape(DEPTH, 16, 256, 512))
        m["cv_in"] = np.ascontiguousarray(f(cache_mem_v)[:, s0:s0 + 16].reshape(DEPTH, 16, 256, 512))
        in_maps.append(m)
    return in_maps


def kernel(**inputs):
    nc = _get_nc(DEPTH, GROUPS)
    in_maps = prepare_inputs(**inputs)
    res = run_bass_kernel_spmd(nc, in_maps, core_ids=list(range(8)))
    R = res.results
    y_prompt = np.stack([R[c]["yp"] for c in range(4)])
    y_sample = np.concatenate([R[c]["ys"].reshape(16, 4, D) for c in range(8)], axis=0)
    hs_p = np.stack([R[c]["hsp"] for c in range(4)], axis=1)
    pb_p = np.stack([R[c]["ppool"] for c in range(4)], axis=1)
    mk_p = np.stack([R[c]["mk_out"].reshape(DEPTH, 256, 4, 128) for c in range(4)], axis=1)
    mv_p = np.stack([R[c]["mv_out"].reshape(DEPTH, 256, 4, 128) for c in range(4)], axis=1)
    hs_s = np.concatenate([R[c]["hss"] for c in range(8)], axis=1)
    pb_s = np.concatenate([R[c]["pools"].reshape(DEPTH, 16, 15, 512) for c in range(8)], axis=1)
    return (y_prompt.astype(np.float32), y_sample.astype(np.float32), hs_p.astype(np.float32), pb_p.astype(np.float32),
            mk_p.astype(np.float32), mv_p.astype(np.float32), hs_s.astype(np.float32), pb_s.astype(np.float32))
```

```python
import numpy as np
from contextlib import ExitStack
import concourse.bass as bass
import concourse.mybir as mybir
from concourse.bass_utils import run_bass_kernel_spmd

F32 = mybir.dt.float32
BF16 = mybir.dt.bfloat16
AF = mybir.ActivationFunctionType
ALU = mybir.AluOpType

D = 2048
KC = 16
DFF = 5632
FC = 44
DEPTH = 4
NH = 8
EPS = 1e-6
CH = 32
TMAX = 704
HT = 352
NSEQ_S = 16
POOL_W = (2, 4, 8, 16)
NSLOT = 6
NGEN = 20
GW = 4 * DEPTH * KC + KC
G2W = DEPTH * 8 + DEPTH * 4 + 32
CW = 128 + 32 + 64 + 16 + 64

GROUPS = [(0, 640, 16), (640, 704, 0), (1344, 704, 0)]


class Prog:
    ENGS = ("pe", "act", "dve", "pool", "sp")

    def __init__(self):
        self.ops = []
        self.lastw = {}
        self.readers = {}
        self.stream_last = {}
        self.gen_i = 0

    def add(self, eng, fn, reads=(), writes=(), dma=False, stream=None):
        oid = len(self.ops)
        deps = {}
        lastw = self.lastw
        readers = self.readers
        psr = [k for k in reads if isinstance(k, tuple) and k[0] == "ps"]
        if psr:
            writes = list(writes) + [k for k in psr if k not in writes]
        for k in reads:
            w = lastw.get(k)
            if w is not None:
                deps[w] = "raw"
        for k in writes:
            w = lastw.get(k)
            if w is not None and w not in deps:
                deps[w] = "waw"
            rs = readers.get(k)
            if rs:
                for r in rs:
                    if r not in deps:
                        deps[r] = "war"
        for k in reads:
            rs = readers.get(k)
            if rs is None:
                readers[k] = [oid]
            else:
                rs.append(oid)
        for k in writes:
            lastw[k] = oid
            readers[k] = []
        if dma:
            if stream is None:
                stream = "g%d" % (self.gen_i % NGEN)
                self.gen_i += 1
            prev = self.stream_last.get(stream)
            if prev is not None and prev not in deps:
                deps[prev] = "ser"
            self.stream_last[stream] = oid
        keep = []
        ops = self.ops
        for d, kind in deps.items():
            dop = ops[d]
            if (not dop[3]) and (not dma) and dop[0] == eng and eng == "pe":
                continue
            keep.append(d)
        self.ops.append([eng, fn, keep, dma, stream, False, 0, (tuple(reads), tuple(writes))])
        return oid

    def emit(self, nc, block, es):
        ops = self.ops
        for op in ops:
            for d in op[2]:
                ops[d][5] = True
        cnt = {e: 0 for e in self.ENGS}
        scnt = {}
        for op in ops:
            if op[3]:
                scnt[op[4]] = scnt.get(op[4], 0) + 16
                op[6] = scnt[op[4]]
            elif op[5]:
                cnt[op[0]] += 1
                op[6] = cnt[op[0]]
        sems = {}
        for e in self.ENGS:
            sems[e] = es.enter_context(nc.semaphore("sem_" + e))
        for s in scnt:
            sems["dma_" + s] = es.enter_context(nc.semaphore("semd_" + s))
        per = {e: [] for e in self.ENGS}
        for op in ops:
            per[op[0]].append(op)
        final_streams = dict(scnt)

        self.log = []

        def run_engine(ename, eng):
            waited = {}
            for op in per[ename]:
                self.log.append((ename, [(("dma_" + ops[d][4]) if ops[d][3] else ops[d][0], ops[d][6]) for d in op[2]], op[3], op[4], op[6], op[7]))
                need = {}
                for d in op[2]:
                    dop = ops[d]
                    if dop[3]:
                        sk = "dma_" + dop[4]
                    else:
                        sk = dop[0]
                    v = dop[6]
                    if v > need.get(sk, 0):
                        need[sk] = v
                for sk, v in need.items():
                    if waited.get(sk, 0) >= v:
                        continue
                    eng.wait_ge(sems[sk], v)
                    waited[sk] = v
                ins = op[1](eng)
                if op[3]:
                    ins.then_inc(sems["dma_" + op[4]], 16)
                elif op[5]:
                    ins.then_inc(sems[ename], 1)
            if ename == "sp":
                for s, v in final_streams.items():
                    if waited.get("dma_" + s, 0) < v:
                        eng.wait_ge(sems["dma_" + s], v)
                for e in ("pe", "act", "dve", "pool"):
                    if cnt[e] > 0:
                        eng.wait_ge(sems[e], cnt[e])

        @block.tensor
        def _(e):
            run_engine("pe", e)

        @block.scalar
        def _(e):
            run_engine("act", e)

        @block.vector
        def _(e):
            run_engine("dve", e)

        @block.gpsimd
        def _(e):
            run_engine("pool", e)

        @block.sync
        def _(e):
            run_engine("sp", e)


class Builder:
    def __init__(self, depth=DEPTH, groups=GROUPS, dbg=False, stages=None):
        self.stages = stages
        self.depth = depth
        self.groups = groups
        self.dbg = dbg
        self.nc = bass.Bass("TRN2", target_bir_lowering=False)
        self.p = Prog()
        self.bank_i = 0
        self.slot_i = 0
        self.rot = {}

    def dram_in(self, name, shape, dt=F32):
        return self.nc.dram_tensor(name, list(shape), dt, kind="ExternalInput").ap()

    def dram_out(self, name, shape, dt=F32):
        return self.nc.dram_tensor(name, list(shape), dt, kind="ExternalOutput").ap()

    def sb(self, es, name, shape, dt):
        return es.enter_context(self.nc.sbuf_tensor(name, list(shape), dt))

    def MM(self, out, lhsT, rhs, start, stop, reads, writes):
        self.p.add("pe", lambda e: e.matmul(out, lhsT, rhs, start=start, stop=stop), reads, writes)

    def TR(self, out, in_, ident, reads, writes):
        self.p.add("pe", lambda e: e.transpose(out, in_, ident), reads, writes)

    def ACT(self, out, in_, func, reads, writes, bias=None, scale=None):
        kw = {}
        if bias is not None:
            kw["bias"] = bias
        if scale is not None:
            kw["scale"] = scale
        self.p.add("act", lambda e: e.activation(out=out, in_=in_, func=func, **kw), reads, writes)

    def TT(self, out, in0, in1, op, reads, writes, eng="dve"):
        self.p.add(eng, lambda e: e.tensor_tensor(out=out, in0=in0, in1=in1, op=op), reads, writes)

    def TS(self, out, in0, s1, s2, op0, op1, reads, writes, eng="dve"):
        if s2 is None:
            self.p.add(eng, lambda e: e.tensor_scalar(out=out, in0=in0, scalar1=s1, scalar2=None, op0=op0), reads, writes)
        else:
            self.p.add(eng, lambda e: e.tensor_scalar(out=out, in0=in0, scalar1=s1, scalar2=s2, op0=op0, op1=op1), reads, writes)

    def STT(self, out, in0, scalar, in1, op0, op1, reads, writes):
        self.p.add("dve", lambda e: e.scalar_tensor_tensor(out=out, in0=in0, scalar=scalar, in1=in1, op0=op0, op1=op1), reads, writes)

    def CP(self, out, in_, reads, writes, eng="dve"):
        if eng == "act":
            self.p.add("act", lambda e: e.copy(out=out, in_=in_), reads, writes)
        else:
            self.p.add(eng, lambda e: e.tensor_copy(out=out, in_=in_), reads, writes)

    def MEMSET(self, ap, val, writes, eng="dve"):
        self.p.add(eng, lambda e: e.memset(ap, val), (), writes)

    def DMA(self, out, in_, reads, writes, q="sp", stream=None):
        self.p.add(q, lambda e: e.dma_start(out=out, in_=in_), reads, writes, dma=True, stream=stream)

    def bank(self):
        b = self.bank_i % 4
        self.bank_i += 1
        return b

    def rotate(self, name, n):
        i = self.rot.get(name, 0)
        self.rot[name] = i + 1
        return i % n

    def wload(self, src, ncols):
        s = self.slot_i % NSLOT
        self.slot_i += 1
        dst = self.wring[:, s, 0:ncols]
        self.DMA(dst, src, (), [("w", s)], q="pool", stream="w%d" % s)
        return s

    def tk(self, name, kc, c0, c1):
        ks = []
        for ti, (a, b) in enumerate(self.tiles):
            if c0 < b and c1 > a:
                ks.append((name, kc, ti))
        return ks

    def build(self):
        nc = self.nc
        L = self.depth
        with ExitStack() as es:
            es.enter_context(nc.allow_non_contiguous_dma(reason="small strided state/param loads"))
            self.declare(es)
            self.prologue()
            for gi, g in enumerate(self.groups):
                self.run_group(gi, g)
            block = es.enter_context(nc.Block())
            self.p.emit(nc, block, es)
        return nc

    def declare(self, es):
        d = self.dram_in
        o = self.dram_out
        self.xp = d("xp", [2048, D])
        self.xs = d("xs", [64, D])
        self.memp = d("memp", [256, D])
        self.hs_in = d("hs_in", [DEPTH, NSEQ_S, NH, 128, 128])
        self.pool_in = d("pool_in", [DEPTH, NSEQ_S * 15, 512])
        self.ck_in = d("ck_in", [DEPTH, NSEQ_S, 256, 512])
        self.cv_in = d("cv_in", [DEPTH, NSEQ_S, 256, 512])
        self.gvec = d("gvec", [128, GW])
        self.gvec2 = d("gvec2", [128, G2W])
        self.consts = d("consts", [128, CW])
        self.w1a = d("w1a", [DEPTH, FC, 128, D])
        self.w3a = d("w3a", [DEPTH, FC, 128, D])
        self.w2a = d("w2a", [DEPTH, KC, 128, DFF])
        self.w1b = d("w1b", [DEPTH, FC, 128, D])
        self.w3b = d("w3b", [DEPTH, FC, 128, D])
        self.w2b = d("w2b", [DEPTH, KC, 128, DFF])
        self.win = d("win", [DEPTH, 88, 128, D])
        self.wbr = d("wbr", [DEPTH, KC, 128, D])
        self.wout = d("wout", [DEPTH, KC, 128, D])
        self.wmk = d("wmk", [DEPTH, 4, 128, D])
        self.wmv = d("wmv", [DEPTH, 4, 128, D])
        self.wpool = d("wpool", [DEPTH, 4, 128, 128])
        self.yp = o("yp", [2048, D])
        self.ys = o("ys", [64, D])
        self.hsp = o("hsp", [DEPTH, NH, 128, 128])
        self.ppool = o("ppool", [DEPTH, 15, 512])
        self.mk_out = o("mk_out", [DEPTH, 256, 512])
        self.mv_out = o("mv_out", [DEPTH, 256, 512])
        self.hss = o("hss", [DEPTH, NSEQ_S, NH, 128, 128])
        self.pools = o("pools", [DEPTH, NSEQ_S, 15, 512])
        self.handS = self.nc.dram_tensor("handS", [DEPTH, NH, 128, 128], F32).ap()
        sb = self.sb
        self.x = sb(es, "x", [128, KC, TMAX], F32)
        self.xn = sb(es, "xn", [128, KC, TMAX], BF16)
        self.obr = sb(es, "obr", [128, KC, TMAX], BF16)
        self.hy = sb(es, "hy", [128, KC, TMAX], BF16)
        self.wring = sb(es, "wring", [128, NSLOT, 2048], BF16)
        self.sf = [sb(es, "sf%d" % i, [128, TMAX + 16], F32) for i in range(5)]
        self.sbb = [sb(es, "sbb%d" % i, [128, TMAX], BF16) for i in range(4)]
        self.tokk = sb(es, "tokk", [64, 2, 4, 128], BF16)
        self.tokv = sb(es, "tokv", [64, 2, 4, 128], BF16)
        self.tokks = sb(es, "tokks", [64, 128], BF16)
        self.tokvs = sb(es, "tokvs", [64, 128], BF16)
        self.stg = sb(es, "stg", [128, 2, 1024], F32)
        self.sq = sb(es, "sq", [128, 2, HT], BF16)
        self.rs = sb(es, "rs", [128, 2, HT], F32)
        self.sa = sb(es, "sa", [128, 2, HT], F32)
        self.g_sb = sb(es, "g_sb", [128, GW], F32)
        self.g2_sb = sb(es, "g2_sb", [128, G2W], F32)
        self.c_sb = sb(es, "c_sb", [128, CW], F32)
        self.identb = sb(es, "identb", [128, 128], BF16)
        self.onesb = sb(es, "onesb", [128, 128], BF16)
        self.epsc = sb(es, "epsc", [128, 1], F32)
        self.rmask = sb(es, "rmask", [128, 768], BF16)
        self.lb = sb(es, "lb", [128, 8, 4], F32)
        self.oml = sb(es, "oml", [128, 8, 4], F32)
        self.lbtmp = sb(es, "lbtmp", [128, 8, 4], F32)
        self.lbs = sb(es, "lbs", [128, 8], F32)
        self.dd = sb(es, "dd", [128, 2, 40], F32)
        self.Sf = sb(es, "Sf", [128, 2, 128], F32)
        self.Sb = sb(es, "Sb", [128, 4, 128], BF16)
        self.atsb = sb(es, "atsb", [64, 2, 64], BF16)
        self.ptail = sb(es, "ptail", [128, DEPTH, 4, 16], F32)
        self.mkT = sb(es, "mkT", [128, 4, 256], BF16)
        self.mvt = sb(es, "mvt", [128, 2, 512], BF16)
        self.ckb = sb(es, "ckb", [128, 2, 2, 512], BF16)
        self.cvb = sb(es, "cvb", [128, 2, 2, 512], BF16)
        self.mkTs = sb(es, "mkTs", [128, 4, 256], BF16)
        self.S0f = sb(es, "S0f", [128, 2, 4, 128], F32)
        self.S0b = sb(es, "S0b", [128, 2, 4, 128], BF16)
        self.kmask = sb(es, "kmask", [64, 2, 4, 128], BF16)
        self.uus = sb(es, "uus", [128, NSEQ_S, 19], F32)
        self.uu2 = sb(es, "uu2", [128, NSEQ_S, 19], F32)
        self.ss1 = sb(es, "ss1", [128, 8], F32)
        self.ss2 = sb(es, "ss2", [128, 2], F32)
        self.ps = [es.enter_context(self.nc.psum_tensor("ps%d" % i, [128, 512], F32)) for i in range(8)]

    def ident_f(self, n=128):
        return self.c_sb[0:n, 0:n]

    def prologue(self):
        self.DMA(self.g_sb[:], self.gvec[:, :], (), ["g_sb"])
        self.DMA(self.g2_sb[:], self.gvec2[:, :], (), ["g2_sb"])
        self.DMA(self.c_sb[:], self.consts[:, :], (), ["c_sb"])
        self.CP(self.identb[:], self.c_sb[:, 0:128], ["c_sb"], ["identb"])
        self.MEMSET(self.onesb[:], 1.0, ["onesb"])
        self.MEMSET(self.epsc[:], EPS, ["epsc"])
        self.MEMSET(self.ptail[:], 0.0, ["ptail"])
        for i in range(5):
            self.MEMSET(self.sf[i][:], 0.0, [("sf", i)])
        for i in range(4):
            self.MEMSET(self.sbb[i][:], 0.0, [("sbb", i)])
        self.MEMSET(self.uus[:], 0.0, ["uus"])
        self.MEMSET(self.uu2[:], 0.0, ["uu2"])
        self.MEMSET(self.dd[:], 0.0, [("dd", 0), ("dd", 1)])
        self.MEMSET(self.rmask[:], 1.0, ["rmask"])
        self.MEMSET(self.rmask[:, 0:704].rearrange("p (c j) -> p c j", j=CH)[:, :, 0:1], 0.0, ["rmask"])
        self.MEMSET(self.rmask[:, 704:768].rearrange("p (c j) -> p c j", j=4)[:, :, 0:1], 0.0, ["rmask"])
        o2 = DEPTH * 8 + DEPTH * 4
        lbl = self.g2_sb[:, o2:o2 + 32].rearrange("p (h l) -> p h l", l=4)
        self.ACT(self.lbtmp[:], lbl, AF.Exp, ["g2_sb"], ["lbtmp"])
        self.p.add("dve", lambda e: e.tensor_reduce(out=self.lbs[:], in_=self.lbtmp[:], axis=mybir.AxisListType.X, op=ALU.add), ["lbtmp"], ["lbs"])
        self.p.add("dve", lambda e: e.reciprocal(out=self.lbs[:], in_=self.lbs[:]), ["lbs"], ["lbs"])
        self.TT(self.lbtmp[:], self.lbtmp[:], self.lbs[:].unsqueeze(2).to_broadcast([128, 8, 4]), ALU.mult, ["lbtmp", "lbs"], ["lbtmp"])
        self.MEMSET(self.lb[:, :, 0:1], 0.0, ["lb"])
        self.CP(self.lb[:, :, 1:2], self.lbtmp[:, :, 1:2], ["lbtmp", "lb"], ["lb"])
        self.TT(self.lb[:, :, 2:3], self.lb[:, :, 1:2], self.lbtmp[:, :, 2:3], ALU.add, ["lb", "lbtmp"], ["lb"])
        self.TT(self.lb[:, :, 3:4], self.lb[:, :, 2:3], self.lbtmp[:, :, 3:4], ALU.add, ["lb", "lbtmp"], ["lb"])
        self.TS(self.oml[:], self.lb[:], -1.0, 1.0, ALU.mult, ALU.add, ["lb"], ["oml"])

    def gcol(self, which, l):
        return {"ffn1": 0, "mix": 1, "ffn2": 2, "mem": 3}[which] * DEPTH * KC + l * KC

    def run_group(self, gi, g):
        P0, PL, NS = g
        self.P0, self.PL, self.NS = P0, PL, NS
        T = PL + 4 * NS
        self.T = T
        h = T // 2
        assert T % 2 == 0 and h <= HT and T <= TMAX and PL % CH == 0
        self.tiles = [(0, h), (h, T)]
        self.first = (P0 == 0)
        self.last = (P0 + PL == 2048)
        on = lambda k: self.stages is None or k in self.stages
        self.on = on
        if on("loadx"):
            self.load_x()
        for l in range(self.depth):
            if on("memkv"):
                self.memkv(l, write_out=(gi == 0))
            if on("ffn1"):
                self.ffn(l, self.w1a, self.w3a, self.w2a, "ffn1")
            if on("mixer"):
                self.mixer(l)
            if on("ffn2"):
                self.ffn(l, self.w1b, self.w3b, self.w2b, "ffn2")
        if on("final"):
            self.final_out()

    def tok_blocks(self, dprompt, dsample):
        blocks = []
        t = 0
        while t < self.PL:
            n = min(128, self.PL - t)
            blocks.append((dprompt[self.P0 + t:self.P0 + t + n, :], t, n))
            t += n
        if self.NS:
            blocks.append((dsample[0:4 * self.NS, :], self.PL, 4 * self.NS))
        return blocks

    def load_x(self):
        for src, t0, n in self.tok_blocks(self.xp, self.xs):
            for hf in range(2):
                si = self.rotate("stg", 2)
                self.DMA(self.stg[0:n, si, :], src[:, hf * 1024:(hf + 1) * 1024], (), [("stg", si)])
                for q in range(2):
                    b = self.bank()
                    for j in range(4):
                        kk = q * 4 + j
                        self.TR(self.ps[b][:, j * 128:j * 128 + n], self.stg[0:n, si, kk * 128:(kk + 1) * 128], self.ident_f(n),
                                [("stg", si), "c_sb"], [("ps", b)])
                    kc0 = hf * 8 + q * 4
                    wk = []
                    for j in range(4):
                        wk += self.tk("x", kc0 + j, t0, t0 + n)
                    src_ps = self.ps[b][:].rearrange("p (j c) -> p j c", c=128)[:, :, 0:n]
                    self.CP(self.x[:, kc0:kc0 + 4, t0:t0 + n], src_ps, [("ps", b)], wk, eng=("act" if q % 2 else "dve"))

    def rstd_tile(self, ti, c0, c1, dst, dkey, scale_n):
        n = c1 - c0
        b = self.bank()
        for kc in range(KC):
            j = self.rotate("sq", 2)
            self.ACT(self.sq[:, j, 0:n], self.x[:, kc, c0:c1], AF.Square, [("x", kc, ti)], [("sq", j)])
            self.MM(self.ps[b][:, 0:n], self.onesb[:], self.sq[:, j, 0:n], kc == 0, kc == KC - 1,
                    [("sq", j), "onesb"], [("ps", b)])
        self.ACT(dst, self.ps[b][:, 0:n], AF.Ln, [("ps", b), "epsc"], [dkey], bias=self.epsc[:], scale=1.0 / scale_n)
        self.ACT(dst, dst, AF.Exp, [dkey], [dkey], scale=-0.5)

    def rmsnorm(self, gbase):
        for ti, (c0, c1) in enumerate(self.tiles):
            n = c1 - c0
            r = self.rotate("rs", 2)
            self.rstd_tile(ti, c0, c1, self.rs[:, r, 0:n], ("rs", r), D)
            for kc in range(KC):
                self.STT(self.xn[:, kc, c0:c1], self.x[:, kc, c0:c1], self.g_sb[:, gbase + kc:gbase + kc + 1], self.rs[:, r, 0:n],
                         ALU.mult, ALU.mult, [("x", kc, ti), ("rs", r), "g_sb"], [("xn", kc, ti)])

    def ffn(self, l, w1, w3, w2, which):
        self.rmsnorm(self.gcol(which, l))
        parts = [(0, 16), (16, 32), (32, 44)]
        for (f0, f1) in parts:
            for f in range(f0, f1):
                s1 = self.wload(w1[l, f], D)
                s3 = self.wload(w3[l, f], D)
                W1 = self.wring[:, s1, :].rearrange("p (k m) -> p k m", m=128)
                W3 = self.wring[:, s3, :].rearrange("p (k m) -> p k m", m=128)
                for ti, (c0, c1) in enumerate(self.tiles):
                    n = c1 - c0
                    ba = self.bank()
                    bb = self.bank()
                    for kc in range(KC):
                        self.MM(self.ps[ba][:, 0:n], W1[:, kc, :], self.xn[:, kc, c0:c1], kc == 0, kc == KC - 1,
                                [("w", s1), ("xn", kc, ti)], [("ps", ba)])
                    for kc in range(KC):
                        self.MM(self.ps[bb][:, 0:n], W3[:, kc, :], self.xn[:, kc, c0:c1], kc == 0, kc == KC - 1,
                                [("w", s3), ("xn", kc, ti)], [("ps", bb)])
                    j = self.rotate("sa", 2)
                    self.ACT(self.sa[:, j, 0:n], self.ps[ba][:, 0:n], AF.Silu, [("ps", ba)], [("sa", j)])
                    self.TT(self.hy[:, f - f0, c0:c1], self.sa[:, j, 0:n], self.ps[bb][:, 0:n], ALU.mult,
                            [("sa", j), ("ps", bb)], [("hy", f - f0, ti)])
            nf = f1 - f0
            for oc in range(KC):
                s2 = self.wload(w2[l, oc][:, f0 * 128:f1 * 128], nf * 128)
                W2 = self.wring[:, s2, 0:nf * 128].rearrange("p (k m) -> p k m", m=128)
                for ti, (c0, c1) in enumerate(self.tiles):
                    n = c1 - c0
                    b = self.bank()
                    for j in range(nf):
                        self.MM(self.ps[b][:, 0:n], W2[:, j, :], self.hy[:, j, c0:c1], j == 0, j == nf - 1,
                                [("w", s2), ("hy", j, ti)], [("ps", b)])
                    self.STT(self.x[:, oc, c0:c1], self.ps[b][:, 0:n], 0.5, self.x[:, oc, c0:c1], ALU.mult, ALU.add,
                             [("ps", b), ("x", oc, ti)], [("x", oc, ti)])

    def memkv(self, l, write_out):
        gb = self.gcol("mem", l)
        for blk in range(2):
            for hf in range(2):
                self.DMA(self.stg[:, hf, :], self.memp[blk * 128:(blk + 1) * 128, hf * 1024:(hf + 1) * 1024], (), [("stg", hf)])
                for qq in range(2):
                    col = blk * 4 + hf * 2 + qq
                    self.p.add("act", lambda e, hf=hf, qq=qq, col=col: e.activation(
                        out=self.sf[4][:, 0:512], in_=self.stg[:, hf, qq * 512:(qq + 1) * 512], func=AF.Square,
                        accum_out=self.ss1[:, col:col + 1]), [("stg", hf)], [("ss1", blk), ("sf", 4)])
            self.p.add("dve", lambda e, blk=blk: e.tensor_reduce(out=self.ss2[:, blk:blk + 1], in_=self.ss1[:, blk * 4:blk * 4 + 4],
                                                                  axis=mybir.AxisListType.X, op=ALU.add), [("ss1", blk)], [("ss2", blk)])
            self.ACT(self.ss2[:, blk:blk + 1], self.ss2[:, blk:blk + 1], AF.Ln, [("ss2", blk), "epsc"], [("ss2", blk)], bias=self.epsc[:], scale=1.0 / D)
            self.ACT(self.ss2[:, blk:blk + 1], self.ss2[:, blk:blk + 1], AF.Exp, [("ss2", blk)], [("ss2", blk)], scale=-0.5)
            for hf in range(2):
                self.TS(self.stg[:, hf, :], self.stg[:, hf, :], self.ss2[:, blk:blk + 1], None, ALU.mult, None,
                        [("stg", hf), ("ss2", blk)], [("stg", hf)])
                for q in range(2):
                    b = self.bank()
                    for j in range(4):
                        kk = q * 4 + j
                        self.TR(self.ps[b][:, j * 128:(j + 1) * 128], self.stg[:, hf, kk * 128:(kk + 1) * 128], self.ident_f(128),
                                [("stg", hf), "c_sb"], [("ps", b)])
                    for j in range(4):
                        kc = hf * 8 + q * 4 + j
                        self.TS(self.obr[:, kc, blk * 128:(blk + 1) * 128], self.ps[b][:, j * 128:(j + 1) * 128],
                                self.g_sb[:, gb + kc:gb + kc + 1], None, ALU.mult, None, [("ps", b), "g_sb"], [("obr", kc, 0)])
        memT = lambda kc, a, b_: self.obr[:, kc, a:b_]
        if not self.on("mk_b"):
            return
        for wsrc, isk in ((self.wmk, True), (self.wmv, False)):
            oi = 0 if isk else 1
            for hh in range(4):
                s = self.wload(wsrc[l, hh], D)
                W = self.wring[:, s, :].rearrange("p (k m) -> p k m", m=128)
                if not self.on("mk_mm"):
                    continue
                if isk:
                    b = self.bank()
                    for kc in range(KC):
                        self.MM(self.ps[b][:, 0:256], W[:, kc, :], memT(kc, 0, 256), kc == 0, kc == KC - 1,
                                [("w", s), ("obr", kc, 0)], [("ps", b)])
                    self.CP(self.mkT[:, hh, :], self.ps[b][:, 0:256], [("ps", b)], ["mkT"], eng="act")
                if (not isk) or write_out:
                    b = self.bank()
                    for blk in range(2):
                        for kc in range(KC):
                            self.MM(self.ps[b][:, blk * 128:(blk + 1) * 128], memT(kc, blk * 128, (blk + 1) * 128), W[:, kc, :],
                                    kc == 0, kc == KC - 1, [("w", s), ("obr", kc, 0)], [("ps", b)])
                    src = self.ps[b][:, 0:256].rearrange("p (a c) -> p a c", c=128)
                    if not isk:
                        self.CP(self.mvt[:, :, hh * 128:(hh + 1) * 128], src, [("ps", b)], ["mvt"], eng="act")
                    if write_out:
                        self.CP(self.stg[:, oi, :].rearrange("p (a c) -> p a c", c=512)[:, :, hh * 128:(hh + 1) * 128], src,
                                [("ps", b)], [("stg", oi)], eng="dve")
            if write_out and self.on("mk_o"):
                dst = (self.mk_out if isk else self.mv_out)[l].rearrange("(a p) c -> p a c", p=128)
                self.DMA(dst, self.stg[:, oi, :].rearrange("p (a c) -> p a c", c=512), [("stg", oi)], [("mko", l, oi)])

    def mixer(self, l):
        on = self.on
        self.rmsnorm(self.gcol("mix", l))
        if on("hgrn"):
            for _ in self.hgrn_A(l, 0):
                pass
            for hd in range(NH):
                g = self.hgrn_A(l, hd + 1) if hd + 1 < NH else iter(())
                self.hgrn_B(l, hd, lambda g=g: next(g, None))
                for _ in g:
                    pass
        if on("pool"):
            self.pool_branch(l)
        if on("xattn"):
            self.xattn(l)
        if on("merge"):
            self.merge(l)

    def proj_fm(self, l, col_tile, evac):
        s = self.wload(self.win[l, col_tile], D)
        W = self.wring[:, s, :].rearrange("p (k m) -> p k m", m=128)
        for ti, (c0, c1) in enumerate(self.tiles):
            n = c1 - c0
            b = self.bank()
            for kc in range(KC):
                self.MM(self.ps[b][:, 0:n], W[:, kc, :], self.xn[:, kc, c0:c1], kc == 0, kc == KC - 1,
                        [("w", s), ("xn", kc, ti)], [("ps", b)])
            evac(ti, c0, c1, self.ps[b][:, 0:n], ("ps", b))

    def psO(self, a, b_):
        return self.ps[4][:, a:b_] if a < 512 else self.ps[5][:, a - 512:b_ - 512]

    def pkO(self, a):
        return ("ps", 4) if a < 512 else ("ps", 5)

    def hbuf(self, par, i):
        if par == 0:
            return self.sbb[i], [("sbb", i)]
        return self.hy[:, i, :], [("hy", i, 0), ("hy", i, 1)]

    def ogbuf(self, par):
        j = 4 + 2 * par
        ap = self.hy[:, j:j + 2, :].rearrange("p a t -> p (a t)").bitcast(F32)
        return ap, [("hy", j, 0), ("hy", j, 1), ("hy", j + 1, 0), ("hy", j + 1, 1)]

    def proj_g(self, l, col_tile, evac):
        s = self.wload(self.win[l, col_tile], D)
        W = self.wring[:, s, :].rearrange("p (k m) -> p k m", m=128)
        for ti, (c0, c1) in enumerate(self.tiles):
            n = c1 - c0
            b = self.bank()
            for kc in range(KC):
                self.MM(self.ps[b][:, 0:n], W[:, kc, :], self.xn[:, kc, c0:c1], kc == 0, kc == KC - 1,
                        [("w", s), ("xn", kc, ti)], [("ps", b)])
            evac(ti, c0, c1, self.ps[b][:, 0:n], ("ps", b))
            yield

    def hgrn_A(self, l, hd):
        PL, NS, T = self.PL, self.NS, self.T
        par = hd % 2
        sf = self.sf
        K = lambda i: ("sf", i)
        (qt, qk), (kt, kk), (kh, khk), (vv, vk) = [self.hbuf(par, i) for i in range(4)]
        og, ogk = self.ogbuf(par)
        dd = self.dd[:, par, :]
        ddk = ("dd", par)
        yield from self.proj_g(l, hd, lambda ti, c0, c1, ps, pk: self.ACT(sf[0][:, c0:c1], ps, AF.Silu, [pk], [K(0)]))
        yield from self.proj_g(l, 24 + hd, lambda ti, c0, c1, ps, pk: self.ACT(og[:, c0:c1], ps, AF.Silu, [pk], ogk))
        yield from self.proj_g(l, 8 + hd, lambda ti, c0, c1, ps, pk: self.ACT(sf[1][:, c0:c1], ps, AF.Sigmoid, [pk], [K(1)]))
        yield from self.proj_g(l, 16 + hd, lambda ti, c0, c1, ps, pk: self.CP(vv[:, c0:c1], ps, [pk], vk, eng="dve"))
        lbc = self.lb[:, hd, l:l + 1]
        omc = self.oml[:, hd, l:l + 1]
        self.TS(sf[1][:, 0:T], sf[1][:, 0:T], omc, lbc, ALU.mult, ALU.add, [K(1), "oml", "lb"], [K(1)])
        yield
        self.TS(sf[2][:, 0:T], sf[1][:, 0:T], -1.0, 1.0, ALU.mult, ALU.add, [K(1)], [K(2)])
        self.ACT(sf[3][:, 0:T], sf[1][:, 0:T], AF.Ln, [K(1)], [K(3)])
        yield
        self.p.add("dve", lambda e: e.tensor_tensor_scan(out=sf[4][:, 0:T], data0=self.rmask[:, 704 - PL:704 - PL + T], data1=sf[3][:, 0:T],
                                                         initial=0.0, op0=ALU.mult, op1=ALU.add), [K(3), "rmask"], [K(4)])
        yield
        self.ACT(sf[3][:, 0:T], sf[4][:, 0:T], AF.Exp, [K(4)], [K(3)])
        yield
        self.TT(qt[:, 0:T], sf[0][:, 0:T], sf[3][:, 0:T], ALU.mult, [K(0), K(3)], qk)
        self.ACT(sf[1][:, 0:T], sf[4][:, 0:T], AF.Exp, [K(4)], [K(1)], scale=-1.0)
        yield
        self.TT(kt[:, 0:T], sf[2][:, 0:T], sf[1][:, 0:T], ALU.mult, [K(2), K(1)], kk)
        yield
        nch = PL // CH
        b3 = sf[4][:, 0:PL].rearrange("p (c j) -> p c j", j=CH)
        self.ACT(dd[:, 0:nch], sf[4][:, CH - 1:PL:CH], AF.Exp, [K(4)], [ddk])
        self.TT(sf[3][:, 0:PL].rearrange("p (c j) -> p c j", j=CH), b3[:, :, CH - 1:CH].to_broadcast([128, nch, CH]), b3, ALU.subtract,
                [K(4), K(3)], [K(3)])
        yield
        if NS:
            b4 = sf[4][:, PL:T].rearrange("p (c j) -> p c j", j=4)
            self.ACT(dd[:, 24:24 + NS], sf[4][:, PL + 3:T:4], AF.Exp, [K(4)], [ddk])
            self.TT(sf[3][:, PL:T].rearrange("p (c j) -> p c j", j=4), b4[:, :, 3:4].to_broadcast([128, NS, 4]), b4, ALU.subtract,
                    [K(4), K(3)], [K(3)])
            yield
        self.ACT(sf[3][:, 0:T], sf[3][:, 0:T], AF.Exp, [K(3)], [K(3)])
        yield
        self.TT(kh[:, 0:T], sf[2][:, 0:T], sf[3][:, 0:T], ALU.mult, [K(2), K(3)], khk)
        yield

    def hgrn_B(self, l, hd, step):
        PL, NS, T = self.PL, self.NS, self.T
        par = hd % 2
        psO, pkO = self.psO, self.pkO
        (qt, qk), (kt, kk), (kh, khk), (vv, vk) = [self.hbuf(par, i) for i in range(4)]
        og, ogk = self.ogbuf(par)
        dd = self.dd[:, par, :]
        ddk = ("dd", par)
        nch = PL // CH
        psT = self.ps[6][:].bitcast(BF16)
        if self.first:
            self.MEMSET(self.Sf[:, 0, :], 0.0, [("Sf", 0)])
            self.MEMSET(self.Sb[:, 0, :], 0.0, [("Sb", 0)])
        else:
            self.DMA(self.Sf[:, 0, :], self.handS[l, hd], [("handS", l, hd)], [("Sf", 0)])
            self.CP(self.Sb[:, 0, :], self.Sf[:, 0, :], [("Sf", 0)], [("Sb", 0)], eng="act")
        cmask = self.c_sb[0:CH, 128:128 + CH]
        cur = 0
        for c in range(nch):
            sl = (c // 4) % 2
            cj = c % 4
            if cj == 0:
                m = min(4, nch - c)
                for (src, dst, dk, bk, off) in ((kh, self.tokk, "tokk", khk, 0), (vv, self.tokv, "tokv", vk, 512)):
                    for j in range(m):
                        self.TR(psT[0:CH, off + j * 128:off + (j + 1) * 128], src[:, (c + j) * CH:(c + j + 1) * CH], self.identb[:],
                                bk + ["identb"], [("ps", 6)])
                    self.CP(dst[0:CH, sl, 0:m, :], psT[0:CH, off:off + m * 128].rearrange("p (a c) -> p a c", c=128), [("ps", 6)], [(dk, sl)],
                            eng=("act" if off else "dve"))
            a0, a1 = c * CH, (c + 1) * CH
            aj = self.rotate("atsb", 2)
            nxt = (cur + 1) % 4
            fc, fn = c % 2, (c + 1) % 2
            bk = self.bank()
            self.MM(self.ps[7][0:CH, 0:CH], kt[:, a0:a1], qt[:, a0:a1], True, True, kk + qk, [("ps", 7)])
            self.MM(self.ps[bk][:, 0:128], self.tokk[0:CH, sl, cj, :], self.tokv[0:CH, sl, cj, :], True, True, [("tokk", sl), ("tokv", sl)], [("ps", bk)])
            self.TT(self.atsb[0:CH, aj, 0:CH], self.ps[7][0:CH, 0:CH], cmask, ALU.mult, [("ps", 7), "c_sb"], [("atsb", aj)])
            self.STT(self.Sf[:, fn, :], self.Sf[:, fc, :], dd[:, c:c + 1], self.ps[bk][:, 0:128], ALU.mult, ALU.add,
                     [("Sf", fc), ddk, ("ps", bk)], [("Sf", fn)])
            self.CP(self.Sb[:, nxt, :], self.Sf[:, fn, :], [("Sf", fn)], [("Sb", nxt)], eng="act")
            self.MM(psO(a0, a1), self.tokv[0:CH, sl, cj, :], self.atsb[0:CH, aj, 0:CH], True, False, [("tokv", sl), ("atsb", aj)], [pkO(a0)])
            self.MM(psO(a0, a1), self.Sb[:, cur, :], qt[:, a0:a1], False, True, [("Sb", cur)] + qk, [pkO(a0)])
            cur = nxt
            step()
        fcur = nch % 2
        if self.last:
            self.DMA(self.hsp[l, hd], self.Sf[:, fcur, :], [("Sf", fcur)], [("hsp", l, hd)])
        else:
            self.DMA(self.handS[l, hd], self.Sf[:, fcur, :], [("Sf", fcur)], [("handS", l, hd)])
        if NS:
            n4 = 4 * NS
            smask = self.c_sb[0:n4, 160:160 + n4]
            for (src, dst, dk, bk) in ((kh, self.tokks, "tokks", khk), (vv, self.tokvs, "tokvs", vk)):
                self.TR(psT[0:n4, 0:128], src[:, PL:T], self.identb[:], bk + ["identb"], [("ps", 6)])
                self.CP(dst[0:n4, :], psT[0:n4, 0:128], [("ps", 6)], [dk], eng="act")
            aj = self.rotate("atsb", 2)
            self.MM(self.ps[7][0:n4, 0:n4], kt[:, PL:T], qt[:, PL:T], True, True, kk + qk, [("ps", 7)])
            self.TT(self.atsb[0:n4, aj, 0:n4], self.ps[7][0:n4, 0:n4], smask, ALU.mult, [("ps", 7), "c_sb"], [("atsb", aj)])
            self.MM(psO(PL, T), self.tokvs[0:n4, :], self.atsb[0:n4, aj, 0:n4], True, False, ["tokvs", ("atsb", aj)], [pkO(PL)])
            for bt in range(NS // 4):
                sl = self.rotate("S0", 2)
                self.DMA(self.S0f[:, sl, :, :], self.hs_in[l, bt * 4:bt * 4 + 4, hd].rearrange("s k v -> k s v"), (), [("S0f", sl)])
                self.CP(self.S0b[:, sl, :, :], self.S0f[:, sl, :, :], [("S0f", sl)], [("S0b", sl)], eng="act")
                for j in range(4):
                    s = bt * 4 + j
                    self.MM(psO(PL + 4 * s, PL + 4 * s + 4), self.S0b[:, sl, j, :], qt[:, PL + 4 * s:PL + 4 * s + 4], False, s == NS - 1,
                            [("S0b", sl)] + qk, [pkO(PL)])
                selm = self.c_sb[0:n4, 224 + bt * 4:224 + bt * 4 + 4]
                self.TT(self.kmask[0:n4, sl, :, :], self.tokks[0:n4, :].unsqueeze(1).to_broadcast([n4, 4, 128]),
                        selm.unsqueeze(2).to_broadcast([n4, 4, 128]), ALU.mult, ["tokks", "c_sb"], [("kmask", sl)])
                b = self.bank()
                for j in range(4):
                    self.MM(self.ps[b][:, j * 128:(j + 1) * 128], self.kmask[0:n4, sl, j, :], self.tokvs[0:n4, :], True, True,
                            [("kmask", sl), "tokvs"], [("ps", b)])
                for j in range(4):
                    s = bt * 4 + j
                    self.STT(self.S0f[:, sl, j, :], self.S0f[:, sl, j, :], dd[:, 24 + s:25 + s], self.ps[b][:, j * 128:(j + 1) * 128],
                             ALU.mult, ALU.add, [("S0f", sl), ddk, ("ps", b)], [("S0f", sl)])
                self.DMA(self.hss[l, bt * 4:bt * 4 + 4, hd].rearrange("s k v -> k s v"), self.S0f[:, sl, :, :], [("S0f", sl)], [("hss", l, hd, bt)])
                step()
        hgc = self.g2_sb[:, l * 8 + hd:l * 8 + hd + 1]
        for ti, (c0, c1) in enumerate(self.tiles):
            segs = []
            a = c0
            while a < c1:
                e_ = min(c1, 512) if a < 512 else c1
                segs.append((a, e_))
                a = e_
            for (a, e_) in segs:
                n = e_ - a
                j = self.rotate("sq", 2)
                self.ACT(self.sq[:, j, 0:n], psO(a, e_), AF.Square, [pkO(a)], [("sq", j)])
                b = self.bank()
                self.MM(self.ps[b][:, 0:n], self.onesb[:], self.sq[:, j, 0:n], True, True, [("sq", j), "onesb"], [("ps", b)])
                r = self.rotate("rs", 2)
                self.ACT(self.rs[:, r, 0:n], self.ps[b][:, 0:n], AF.Ln, [("ps", b), "epsc"], [("rs", r)], bias=self.epsc[:], scale=1.0 / 128)
                self.ACT(self.rs[:, r, 0:n], self.rs[:, r, 0:n], AF.Exp, [("rs", r)], [("rs", r)], scale=-0.5)
                self.STT(self.rs[:, r, 0:n], psO(a, e_), hgc, self.rs[:, r, 0:n], ALU.mult, ALU.mult, [pkO(a), ("rs", r), "g2_sb"], [("rs", r)])
                self.TT(self.obr[:, hd, a:e_], self.rs[:, r, 0:n], og[:, a:e_], ALU.mult, [("rs", r)] + ogk, self.tk("obr", hd, a, e_))
                step()

    def pool_branch(self, l):
        PL, NS, T = self.PL, self.NS, self.T
        sf = self.sf
        K = lambda i: ("sf", i)
        if NS:
            for half in range(2):
                self.DMA(self.stg[0:120, half, 0:512], self.pool_in[l, half * 120:(half + 1) * 120, :], (), [("stg", half)])
        for g in range(4):
            w = POOL_W[g]
            pscol = self.g2_sb[:, DEPTH * 8 + l * 4 + g:DEPTH * 8 + l * 4 + g + 1]

            def ev(ti, c0, c1, ps, pk):
                p1 = min(c1, PL)
                if c0 < p1:
                    self.CP(sf[0][:, 15 + c0:15 + p1], ps[:, 0:p1 - c0], [pk], [K(0)], eng="act")
                if NS and c1 > PL:
                    s0 = max(c0, PL)
                    self.CP(self.uus[:, (s0 - PL) // 4:(c1 - PL) // 4, 15:19],
                            ps[:, s0 - c0:c1 - c0].rearrange("p (s j) -> p s j", j=4), [pk], ["uus"], eng="act")
            if NS:
                b = self.bank()
                for half in range(2):
                    self.TR(self.ps[b][:, half * 128:half * 128 + 120], self.stg[0:120, half, g * 128:(g + 1) * 128], self.ident_f(120),
                            [("stg", half), "c_sb"], [("ps", b)])
                for half in range(2):
                    self.CP(self.uus[:, half * 8:(half + 1) * 8, 0:15],
                            self.ps[b][:, half * 128:half * 128 + 120].rearrange("p (s r) -> p s r", r=15),
                            [("ps", b)], ["uus"], eng="dve")
            if self.first:
                self.MEMSET(sf[0][:, 0:15], 0.0, [K(0)])
            else:
                self.CP(sf[0][:, 0:15], self.ptail[:, l, g, 0:15], ["ptail"], [K(0)])
            self.proj_fm(l, 32 + g, ev)
            NP = 15 + PL
            bufs = [0, 1, 2, 1, 2]
            step = 1
            cur = 0
            for it in range(g + 1):
                nxt = bufs[it + 1]
                self.TT(sf[nxt][:, step:NP], sf[cur][:, step:NP], sf[cur][:, 0:NP - step], ALU.add, [K(cur)], [K(nxt)])
                cur = nxt
                step *= 2
            self.STT(self.sbb[0][:, 0:PL], sf[cur][:, 15:NP], 1.0 / w, sf[0][:, 15:NP], ALU.mult, ALU.subtract, [K(cur), K(0)], [("sbb", 0)])
            if self.first:
                ic = self.c_sb[:, 240 + g * 16:240 + g * 16 + 16]
                self.TT(sf[3][:, 0:16], sf[cur][:, 15:31], ic, ALU.mult, [K(cur), "c_sb"], [K(3)])
                self.TT(self.sbb[0][:, 0:16], sf[3][:, 0:16], sf[0][:, 15:31], ALU.subtract, [K(3), K(0)], [("sbb", 0)])
            if self.last:
                b = self.bank()
                self.TR(self.ps[b][0:32, 0:128], sf[0][:, 15 + PL - 32:15 + PL], self.ident_f(128), [K(0), "c_sb"], [("ps", b)])
                self.CP(self.stg[0:32, 0, 512 + g * 128:512 + (g + 1) * 128], self.ps[b][0:32, 0:128], [("ps", b)], [("stg", 0)], eng="dve")
            else:
                self.CP(self.ptail[:, l, g, 0:15], sf[0][:, PL:PL + 15], [K(0)], ["ptail"])
            sW = self.wload(self.wpool[l, g], 128)
            for ti, (c0, c1) in enumerate(self.tiles):
                p1 = min(c1, PL)
                if c0 >= p1:
                    continue
                n = p1 - c0
                b = self.bank()
                self.MM(self.ps[b][:, 0:n], self.wring[:, sW, 0:128], self.sbb[0][:, c0:p1], True, True, [("w", sW), ("sbb", 0)], [("ps", b)])
                self.TS(self.obr[:, 8 + g, c0:p1], self.ps[b][:, 0:n], pscol, None, ALU.mult, None, [("ps", b), "g2_sb"],
                        self.tk("obr", 8 + g, c0, p1))
            if NS:
                self.pool_sample(l, g, sW, pscol)
        if self.last:
            self.DMA(self.ppool[l], self.stg[17:32, 0, 512:1024], [("stg", 0)], [("ppool", l)])
        if NS:
            self.DMA(self.pools[l, :, 0:11, :], self.pool_in[l].rearrange("(s r) c -> s r c", r=15)[:, 4:15, :], (), [("pools_a", l)])
            for j in range(4):
                self.DMA(self.pools[l, :, 11 + j, :], self.stg[j:4 * NS:4, 1, 512:1024], [("stg", 1)], [("pools_b", l, j)])

    def pool_sample(self, l, g, sW, pscol):
        PL, NS, T = self.PL, self.NS, self.T
        w = POOL_W[g]
        t1 = self.uu2
        t2 = self.sf[3][:, 0:NS * 19].rearrange("p (s r) -> p s r", r=19)
        seq = [self.uus, t1, t2, t1, t2]
        keys = ["uus", "uu2", ("sf", 3), "uu2", ("sf", 3)]
        step = 1
        cur = 0
        for it in range(g + 1):
            nx = it + 1
            self.TT(seq[nx][:, :, step:19], seq[cur][:, :, step:19], seq[cur][:, :, 0:19 - step], ALU.add, [keys[cur]], [keys[nx]])
            cur = nx
            step *= 2
        n = 4 * NS
        dst = self.sbb[1][:, 0:n].rearrange("p (s j) -> p s j", j=4)
        self.STT(dst, seq[cur][:, :, 15:19], 1.0 / w, self.uus[:, :, 15:19], ALU.mult, ALU.subtract, [keys[cur], "uus"], [("sbb", 1)])
        b = self.bank()
        self.MM(self.ps[b][:, 0:n], self.wring[:, sW, 0:128], self.sbb[1][:, 0:n], True, True, [("w", sW), ("sbb", 1)], [("ps", b)])
        self.TS(self.obr[:, 8 + g, PL:T], self.ps[b][:, 0:n], pscol, None, ALU.mult, None, [("ps", b), "g2_sb"], self.tk("obr", 8 + g, PL, T))
        j = self.rotate("rs", 2)
        self.CP(self.rs[:, j, 0:n].rearrange("p (s j) -> p s j", j=4), self.uus[:, :, 15:19], ["uus"], [("rs", j)])
        b = self.bank()
        self.TR(self.ps[b][0:n, 0:128], self.rs[:, j, 0:n], self.ident_f(128), [("rs", j), "c_sb"], [("ps", b)])
        self.CP(self.stg[0:n, 1, 512 + g * 128:512 + (g + 1) * 128], self.ps[b][0:n, 0:128], [("ps", b)], [("stg", 1)], eng="dve")

    def xattn(self, l):
        PL, NS, T = self.PL, self.NS, self.T
        sc = 128 ** -0.5
        for hh in range(4):
            self.proj_fm(l, 36 + hh, lambda ti, c0, c1, ps, pk: self.ACT(self.sbb[0][:, c0:c1], ps, AF.Copy, [pk], [("sbb", 0)], scale=sc))
            if NS:
                self.CP(self.sbb[3][:, hh * 64:hh * 64 + 4 * NS], self.sbb[0][:, PL:T], [("sbb", 0)], [("sbb", 3)])
            for ti, (c0, c1) in enumerate(self.tiles):
                p1 = min(c1, PL)
                if c0 >= p1:
                    continue
                n = p1 - c0
                b0 = self.bank()
                b1 = self.bank()
                for mc, b in ((0, b0), (1, b1)):
                    self.MM(self.ps[b][:, 0:n], self.mkT[:, hh, mc * 128:(mc + 1) * 128], self.sbb[0][:, c0:p1], True, True,
                            ["mkT", ("sbb", 0)], [("ps", b)])
                e0 = self.sbb[1]
                e1 = self.sbb[2]
                self.ACT(e0[:, 0:n], self.ps[b0][:, 0:n], AF.Exp, [("ps", b0)], [("sbb", 1)])
                self.ACT(e1[:, 0:n], self.ps[b1][:, 0:n], AF.Exp, [("ps", b1)], [("sbb", 2)])
                bs = self.bank()
                self.MM(self.ps[bs][:, 0:n], self.onesb[:], e0[:, 0:n], True, False, ["onesb", ("sbb", 1)], [("ps", bs)])
                self.MM(self.ps[bs][:, 0:n], self.onesb[:], e1[:, 0:n], False, True, ["onesb", ("sbb", 2)], [("ps", bs)])
                bo = self.bank()
                self.MM(self.ps[bo][:, 0:n], self.mvt[:, 0, hh * 128:(hh + 1) * 128], e0[:, 0:n], True, False, ["mvt", ("sbb", 1)], [("ps", bo)])
                self.MM(self.ps[bo][:, 0:n], self.mvt[:, 1, hh * 128:(hh + 1) * 128], e1[:, 0:n], False, True, ["mvt", ("sbb", 2)], [("ps", bo)])
                r = self.rotate("rs", 2)
                self.p.add("dve", lambda e, r=r, bs=bs, n=n: e.reciprocal(out=self.rs[:, r, 0:n], in_=self.ps[bs][:, 0:n]), [("ps", bs)], [("rs", r)])
                self.TT(self.obr[:, 12 + hh, c0:p1], self.ps[bo][:, 0:n], self.rs[:, r, 0:n], ALU.mult, [("ps", bo), ("rs", r)],
                        self.tk("obr", 12 + hh, c0, p1))
        if NS:
            self.xattn_sample(l)

    def xattn_sample(self, l):
        PL, NS, T = self.PL, self.NS, self.T
        psS = self.ps[6]
        psS5 = psS[:].rearrange("p (s h c t) -> p s h c t", s=NS, h=4, c=2)
        psT = self.ps[7][:].bitcast(BF16)
        e5 = self.sbb[1][:, 0:NS * 32].rearrange("p (s h c t) -> p s h c t", s=NS, h=4, c=2)
        for s in range(NS):
            j = self.rotate("ckb", 2)
            self.DMA(self.ckb[:, j, :, :], self.ck_in[l, s].rearrange("(a p) c -> p a c", p=128), (), [("ckb", j)], q="pool", stream="ck%d" % j)
            self.DMA(self.cvb[:, j, :, :], self.cv_in[l, s].rearrange("(a p) c -> p a c", p=128), (), [("cvb", j)], q="pool", stream="cv%d" % j)
            for hh in range(4):
                for mc in range(2):
                    self.TR(psT[:, (hh * 2 + mc) * 128:(hh * 2 + mc + 1) * 128], self.ckb[:, j, mc, hh * 128:(hh + 1) * 128], self.identb[:],
                            [("ckb", j), "identb"], [("ps", 7)])
            self.CP(self.mkTs[:, :, :], psT[:, 0:1024].rearrange("p (h m) -> p h m", m=256), [("ps", 7)], ["mkTs"], eng=("act" if s % 2 else "dve"))
            for hh in range(4):
                for mc in range(2):
                    self.MM(psS5[:, s, hh, mc, :], self.mkTs[:, hh, mc * 128:(mc + 1) * 128],
                            self.sbb[3][:, hh * 64 + 4 * s:hh * 64 + 4 * s + 4], True, True, ["mkTs", ("sbb", 3)], [("ps", 6)])
            self.ACT(self.sbb[1][:, s * 32:(s + 1) * 32], psS[:, s * 32:(s + 1) * 32], AF.Exp, [("ps", 6)], [("sbb", 1)])
            for hh in range(4):
                for mc in range(2):
                    self.MM(self.ps[5][:, 256 + s * 16 + hh * 4:256 + s * 16 + hh * 4 + 4], self.cvb[:, j, mc, hh * 128:(hh + 1) * 128],
                            e5[:, s, hh, mc, :], mc == 0, mc == 1, [("cvb", j), ("sbb", 1)], [("ps", 5)])
        b = self.bank()
        self.CP(self.sbb[2][:, 0:NS * 16].rearrange("p (s h t) -> p s h t", s=NS, h=4), e5[:, :, :, 0, :], [("sbb", 1)], [("sbb", 2)])
        self.CP(self.sbb[2][:, 256:256 + NS * 16].rearrange("p (s h t) -> p s h t", s=NS, h=4), e5[:, :, :, 1, :], [("sbb", 1)], [("sbb", 2)])
        self.MM(self.ps[b][:, 0:NS * 16], self.onesb[:], self.sbb[2][:, 0:NS * 16], True, False, ["onesb", ("sbb", 2)], [("ps", b)])
        self.MM(self.ps[b][:, 0:NS * 16], self.onesb[:], self.sbb[2][:, 256:256 + NS * 16], False, True, ["onesb", ("sbb", 2)], [("ps", b)])
        r = self.rotate("rs", 2)
        self.p.add("dve", lambda e: e.reciprocal(out=self.rs[:, r, 0:NS * 16], in_=self.ps[b][:, 0:NS * 16]), [("ps", b)], [("rs", r)])
        for hh in range(4):
            src = self.ps[5][:, 256:256 + NS * 16].rearrange("p (s h t) -> p s h t", s=NS, h=4)[:, :, hh, :]
            rr = self.rs[:, r, 0:NS * 16].rearrange("p (s h t) -> p s h t", s=NS, h=4)[:, :, hh, :]
            self.TT(self.obr[:, 12 + hh, PL:T].rearrange("p (s t) -> p s t", t=4), src, rr, ALU.mult, [("ps", 5), ("rs", r)],
                    self.tk("obr", 12 + hh, PL, T))

    def merge(self, l):
        segs = ((0, 8), (8, 12), (12, 16))
        acc = self.sf[0]
        tmp = self.sf[1]
        for oc in range(KC):
            for br in range(3):
                k0, k1 = segs[br]
                nk = k1 - k0
                sG = self.wload(self.win[l, 40 + br * 16 + oc], D)
                sB = self.wload(self.wbr[l, oc][:, k0 * 128:k1 * 128], nk * 128)
                WG = self.wring[:, sG, :].rearrange("p (k m) -> p k m", m=128)
                WB = self.wring[:, sB, 0:nk * 128].rearrange("p (k m) -> p k m", m=128)
                for ti, (c0, c1) in enumerate(self.tiles):
                    n = c1 - c0
                    bg = self.bank()
                    for kc in range(KC):
                        self.MM(self.ps[bg][:, 0:n], WG[:, kc, :], self.xn[:, kc, c0:c1], kc == 0, kc == KC - 1,
                                [("w", sG), ("xn", kc, ti)], [("ps", bg)])
                    bp = self.bank()
                    for kk in range(nk):
                        self.MM(self.ps[bp][:, 0:n], WB[:, kk, :], self.obr[:, k0 + kk, c0:c1], kk == 0, kk == nk - 1,
                                [("w", sB), ("obr", k0 + kk, ti)], [("ps", bp)])
                    j = self.rotate("sa", 2)
                    self.ACT(self.sa[:, j, 0:n], self.ps[bg][:, 0:n], AF.Sigmoid, [("ps", bg)], [("sa", j)])
                    if br == 0:
                        self.TT(acc[:, c0:c1], self.sa[:, j, 0:n], self.ps[bp][:, 0:n], ALU.mult, [("sa", j), ("ps", bp)], [("sf", 0)])
                    else:
                        self.TT(tmp[:, c0:c1], self.sa[:, j, 0:n], self.ps[bp][:, 0:n], ALU.mult, [("sa", j), ("ps", bp)], [("sf", 1)])
                        if br == 1:
                            self.TT(acc[:, c0:c1], acc[:, c0:c1], tmp[:, c0:c1], ALU.add, [("sf", 0), ("sf", 1)], [("sf", 0)])
                        else:
                            self.TT(self.hy[:, oc, c0:c1], acc[:, c0:c1], tmp[:, c0:c1], ALU.add, [("sf", 0), ("sf", 1)], [("hy", oc, ti)])
        for oc in range(KC):
            s = self.wload(self.wout[l, oc], D)
            W = self.wring[:, s, :].rearrange("p (k m) -> p k m", m=128)
            for ti, (c0, c1) in enumerate(self.tiles):
                n = c1 - c0
                b = self.bank()
                for kc in range(KC):
                    self.MM(self.ps[b][:, 0:n], W[:, kc, :], self.hy[:, kc, c0:c1], kc == 0, kc == KC - 1,
                            [("w", s), ("hy", kc, ti)], [("ps", b)])
                self.TT(self.x[:, oc, c0:c1], self.ps[b][:, 0:n], self.x[:, oc, c0:c1], ALU.add, [("ps", b), ("x", oc, ti)], [("x", oc, ti)])

    def final_out(self):
        PL, NS, T = self.PL, self.NS, self.T
        on = self.on
        gb = 4 * DEPTH * KC
        if on("f_rstd"):
            for ti, (c0, c1) in enumerate(self.tiles):
                self.rstd_tile(ti, c0, c1, self.sf[0][:, c0:c1], ("sf", 0), D)
        ynb = self.stg[:, 0, :].rearrange("p (k c) -> p k c", c=128)
        for bi, (dst, t0, n) in enumerate(self.tok_blocks(self.yp, self.ys)):
            for hf in range(2):
                if on("f_stt"):
                    for kk in range(8):
                        kc = hf * 8 + kk
                        self.STT(ynb[:, kk, 0:n], self.x[:, kc, t0:t0 + n], self.g_sb[:, gb + kc:gb + kc + 1], self.sf[0][:, t0:t0 + n],
                                 ALU.mult, ALU.mult, self.tk("x", kc, t0, t0 + n) + [("sf", 0), "g_sb"], [("stg", 0)])
                for q in range(2):
                    b = self.bank()
                    if on("f_tr"):
                        for j in range(4):
                            kk = q * 4 + j
                            self.TR(self.ps[b][0:n, j * 128:(j + 1) * 128], ynb[:, kk, 0:n], self.ident_f(128), [("stg", 0), "c_sb"], [("ps", b)])
                    if on("f_cp"):
                        self.CP(self.stg[0:n, 1, q * 512:(q + 1) * 512], self.ps[b][0:n, :], [("ps", b)], [("stg", 1)], eng=("act" if q % 2 else "dve"))
                if on("f_dma"):
                    self.DMA(dst[:, hf * 1024:(hf + 1) * 1024], self.stg[0:n, 1, :], [("stg", 1)], [("yout", self.P0, bi, hf)])


def _fm_tiles(w, kc):
    K, M = w.shape
    mc = M // 128
    return np.ascontiguousarray(w.reshape(kc, 128, mc, 128).transpose(2, 1, 0, 3)).reshape(mc, 128, kc * 128)


def _fm_tiles_L(w, kc):
    return np.stack([_fm_tiles(w[l], kc) for l in range(w.shape[0])])


def _vec_fm(v):
    lead = v.shape[:-1]
    n = v.shape[-1] // 128
    a = v.reshape(*lead, n, 128)
    return np.ascontiguousarray(np.moveaxis(a, -1, 0))


def _consts():
    c = np.zeros((128, CW), np.float32)
    c[:, 0:128] = np.eye(128, dtype=np.float32)
    j = np.arange(32)[:, None]
    t = np.arange(32)[None, :]
    c[0:32, 128:160] = (j <= t).astype(np.float32)
    j = np.arange(64)[:, None]
    t = np.arange(64)[None, :]
    c[0:64, 160:224] = ((j <= t) & (j // 4 == t // 4)).astype(np.float32)
    c[0:64, 224:240] = (np.arange(64)[:, None] // 4 == np.arange(16)[None, :]).astype(np.float32)
    for g, w in enumerate(POOL_W):
        c[:, 240 + g * 16:240 + g * 16 + 16] = 1.0 / np.minimum(w, np.arange(16) + 1.0)
    return c


_NC_CACHE = {}


def _get_nc(depth, groups):
    key = (depth, tuple(groups))
    if key not in _NC_CACHE:
        _NC_CACHE[key] = Builder(depth=depth, groups=groups).build()
    return _NC_CACHE[key]


def prepare_inputs(x_prompt, x_sample, state_hgrn, state_pool, cache_mem_k, cache_mem_v, mem_prompt,
                   ffn1_norm, ffn1_w1, ffn1_w3, ffn1_w2, mix_norm, w_in, lb_logits, hg_norm, w_pool,
                   pool_scale, mem_norm, w_mk, w_mv, w_branch, w_out, ffn2_norm, ffn2_w1, ffn2_w3,
                   ffn2_w2, final_norm, cores=range(8)):
    f = lambda a: np.asarray(a, dtype=np.float32)
    gv = np.concatenate([_vec_fm(f(ffn1_norm)).reshape(128, -1), _vec_fm(f(mix_norm)).reshape(128, -1),
                         _vec_fm(f(ffn2_norm)).reshape(128, -1), _vec_fm(f(mem_norm)).reshape(128, -1),
                         _vec_fm(f(final_norm)).reshape(128, -1)], axis=1)
    assert gv.shape == (128, GW)
    lbl = _vec_fm(f(lb_logits))
    lbl = np.ascontiguousarray(lbl.transpose(0, 2, 1)).reshape(128, 32)
    gv2 = np.concatenate([_vec_fm(f(hg_norm)).reshape(128, -1), _vec_fm(f(pool_scale)).reshape(128, -1), lbl], axis=1)
    assert gv2.shape == (128, G2W)
    shared = {
        "gvec": np.ascontiguousarray(gv), "gvec2": np.ascontiguousarray(gv2), "consts": _consts(),
        "w1a": _fm_tiles_L(f(ffn1_w1), KC), "w3a": _fm_tiles_L(f(ffn1_w3), KC), "w2a": _fm_tiles_L(f(ffn1_w2), FC),
        "w1b": _fm_tiles_L(f(ffn2_w1), KC), "w3b": _fm_tiles_L(f(ffn2_w3), KC), "w2b": _fm_tiles_L(f(ffn2_w2), FC),
        "win": _fm_tiles_L(f(w_in), KC), "wbr": _fm_tiles_L(f(w_branch), KC), "wout": _fm_tiles_L(f(w_out), KC),
        "wmk": _fm_tiles_L(f(w_mk), KC), "wmv": _fm_tiles_L(f(w_mv), KC),
        "wpool": np.ascontiguousarray(f(w_pool)),
    }
    xp = f(x_prompt)
    xs = f(x_sample)
    in_maps = []
    for c in cores:
        sq = c % 4
        s0 = 16 * c
        m = dict(shared)
        if c < 4:
            m["xp"] = np.ascontiguousarray(xp[sq])
            m["memp"] = np.ascontiguousarray(f(mem_prompt)[sq])
        else:
            m["xp"] = np.zeros((2048, D), np.float32)
            m["memp"] = np.zeros((256, D), np.float32)
        m["xs"] = np.ascontiguousarray(xs[s0:s0 + 16].reshape(64, D))
        m["hs_in"] = np.ascontiguousarray(f(state_hgrn)[:, s0:s0 + 16])
        m["pool_in"] = np.ascontiguousarray(f(state_pool)[:, s0:s0 + 16].reshape(DEPTH, 16 * 15, 512))
        m["ck_in"] = np.ascontiguousarray(f(cache_mem_k)[:, s0:s0 + 16].reshape(DEPTH, 16, 256, 512))
        m["cv_in"] = np.ascontiguousarray(f(cache_mem_v)[:, s0:s0 + 16].reshape(DEPTH, 16, 256, 512))
        in_maps.append(m)
    return in_maps


def kernel(**inputs):
    nc = _get_nc(DEPTH, GROUPS)
    in_maps = prepare_inputs(**inputs)
    res = run_bass_kernel_spmd(nc, in_maps, core_ids=list(range(8)))
    R = res.results
    y_prompt = np.stack([R[c]["yp"] for c in range(4)])
    y_sample = np.concatenate([R[c]["ys"].reshape(16, 4, D) for c in range(8)], axis=0)
    hs_p = np.stack([R[c]["hsp"] for c in range(4)], axis=1)
    pb_p = np.stack([R[c]["ppool"] for c in range(4)], axis=1)
    mk_p = np.stack([R[c]["mk_out"].reshape(DEPTH, 256, 4, 128) for c in range(4)], axis=1)
    mv_p = np.stack([R[c]["mv_out"].reshape(DEPTH, 256, 4, 128) for c in range(4)], axis=1)
    hs_s = np.concatenate([R[c]["hss"] for c in range(8)], axis=1)
    pb_s = np.concatenate([R[c]["pools"].reshape(DEPTH, 16, 15, 512) for c in range(8)], axis=1)
    return (y_prompt.astype(np.float32), y_sample.astype(np.float32), hs_p.astype(np.float32), pb_p.astype(np.float32),
            mk_p.astype(np.float32), mv_p.astype(np.float32), hs_s.astype(np.float32), pb_s.astype(np.float32))
```
